# Optimizing a Trainium2 kernel written in Bass

```python
import jax
import jax.numpy as jnp
from jax import lax
import numpy as np


D_MODEL = 2048
BATCH = 4
SEQ = 4096
DEPTH = 4

HEAD_DIM = 128
ROPE_THETA = 500000.0
ROPE_DIM = HEAD_DIM // 4
N_MEM = 256
X_HEADS = 4
MOBA_HEADS = 8
MOBA_BLOCK = 256
MOBA_TOPK = 3
MOBA_QCHUNK = 32
NSA_HEADS = 8
NSA_GROUPS = 2
NSA_CMP_LEN = 32
NSA_CMP_STRIDE = 16
NSA_SEL_LEN = 64
NSA_TOPK = 16
NSA_WINDOW = 512
NSA_QCHUNK = 64
WIN_QBLOCK = 128
RET_HEADS = 8
RET_DK = 256
RET_DV = 512
RET_CHUNK = 128
RET_THETA = 10000.0
D_FF = 5632
CONV_WIDTH = 3
N_EVEN = (DEPTH + 1) // 2
N_ODD = DEPTH // 2
RMS_EPS = 1e-6
NEG = -1e30
FORCE = 1e9

AB_SIZES = (MOBA_HEADS * HEAD_DIM,) * 3 + (NSA_HEADS * HEAD_DIM,) + (NSA_GROUPS * HEAD_DIM,) * 6 + (NSA_HEADS * 3,)
AB_COLS = sum(AB_SIZES)
C_SIZES = (RET_HEADS * RET_DK,) * 2 + (RET_HEADS * RET_DV,) * 2
C_COLS = sum(C_SIZES)

kernel_name = 'hybrid_moba_nsa_retention_convffn'


def _split(t, sizes):
    out, start = [], 0
    for s in sizes:
        out.append(t[..., start:start + s])
        start += s
    return out


def _rmsnorm(x, g):
    xf = x.astype(jnp.float32)
    y = xf * lax.rsqrt(jnp.mean(jnp.square(xf), axis=-1, keepdims=True) + RMS_EPS)
    return (y * g.astype(jnp.float32)).astype(x.dtype)


def _heads(t, n):
    B, S, _ = t.shape
    return t.reshape(B, S, n, -1).transpose(0, 2, 1, 3)


def _merge(t):
    B, H, S, d = t.shape
    return t.transpose(0, 2, 1, 3).reshape(B, S, H * d)


def _rope(x, positions, rot_dim, theta):
    half = rot_dim // 2
    inv = jnp.float32(theta) ** (-jnp.arange(half, dtype=jnp.float32) / half)
    ang = positions.astype(jnp.float32)[:, None, :, None] * inv
    cos, sin = jnp.cos(ang), jnp.sin(ang)
    xr = x[..., :rot_dim].astype(jnp.float32)
    x1, x2 = xr[..., :half], xr[..., half:]
    rot = jnp.concatenate([x1 * cos - x2 * sin, x1 * sin + x2 * cos], axis=-1).astype(x.dtype)
    if rot_dim == x.shape[-1]:
        return rot
    return jnp.concatenate([rot, x[..., rot_dim:]], axis=-1)


def _moba(q, k, v):
    B, H, S, d = q.shape
    nb = -(-S // MOBA_BLOCK)
    sp = nb * MOBA_BLOCK
    pad = ((0, 0), (0, 0), (0, sp - S), (0, 0))
    q, k, v = jnp.pad(q, pad), jnp.pad(k, pad), jnp.pad(v, pad)
    kb = k.reshape(B, H, nb, MOBA_BLOCK, d)
    vb = v.reshape(B, H, nb, MOBA_BLOCK, d)
    kmean = jnp.mean(kb.astype(jnp.float32), axis=3)
    gate = jnp.einsum('bhsd,bhnd->bhsn', q.astype(jnp.float32), kmean)
    q_blk = jnp.arange(sp) // MOBA_BLOCK
    past = jnp.arange(nb)[None, :] < q_blk[:, None]
    gate = jnp.where(past, gate, NEG)
    kk = min(MOBA_TOPK, nb)
    g_val, g_idx = lax.top_k(gate, kk)
    g_ok = g_val > NEG / 2
    scale = d ** -0.5
    b_i = jnp.arange(B)[:, None, None, None]
    h_i = jnp.arange(H)[None, :, None, None]
    QC = MOBA_QCHUNK

    def chunk(c):
        s0 = c * QC
        qc = lax.dynamic_slice_in_dim(q, s0, QC, axis=2)
        idx = lax.dynamic_slice_in_dim(g_idx, s0, QC, axis=2)
        ok = lax.dynamic_slice_in_dim(g_ok, s0, QC, axis=2)
        k_sel = kb[b_i, h_i, idx]
        v_sel = vb[b_i, h_i, idx]
        own = s0 // MOBA_BLOCK
        k_own = lax.dynamic_index_in_dim(kb, own, axis=2, keepdims=False)
        v_own = lax.dynamic_index_in_dim(vb, own, axis=2, keepdims=False)
        s_sel = jnp.einsum('bhqd,bhqnkd->bhqnk', qc, k_sel).astype(jnp.float32) * scale
        s_sel = jnp.where(ok[..., None], s_sel, NEG).reshape(B, H, QC, kk * MOBA_BLOCK)
        s_own = jnp.einsum('bhqd,bhkd->bhqk', qc, k_own).astype(jnp.float32) * scale
        q_pos = s0 + jnp.arange(QC)
        k_pos = own * MOBA_BLOCK + jnp.arange(MOBA_BLOCK)
        s_own = jnp.where(k_pos[None, :] <= q_pos[:, None], s_own, NEG)
        p = jax.nn.softmax(jnp.concatenate([s_sel, s_own], axis=-1), axis=-1).astype(v.dtype)
        p_sel = p[..., :kk * MOBA_BLOCK].reshape(B, H, QC, kk, MOBA_BLOCK)
        p_own = p[..., kk * MOBA_BLOCK:]
        return (jnp.einsum('bhqnk,bhqnkd->bhqd', p_sel, v_sel)
                + jnp.einsum('bhqk,bhkd->bhqd', p_own, v_own))

    out = lax.map(chunk, jnp.arange(sp // QC))
    out = out.transpose(1, 2, 0, 3, 4).reshape(B, H, sp, d)
    return out[:, :, :S]


def _compress(t, pe, w1, w2):
    B, G, S, d = t.shape
    f = NSA_CMP_LEN // NSA_CMP_STRIDE
    n_sub = S // NSA_CMP_STRIDE
    n_cmp = n_sub - f + 1
    sub = t.reshape(B, G, n_sub, NSA_CMP_STRIDE, d)
    blocks = jnp.concatenate([sub[:, :, i:i + n_cmp] for i in range(f)], axis=3) + pe
    h = jax.nn.silu(blocks.reshape(B, G, n_cmp, NSA_CMP_LEN * d) @ w1)
    return h @ w2


def _nsa(q, k_c, v_c, k_s, v_s, k_w, v_w, gate_logits, positions, pe_k, w1_k, w2_k, pe_v, w1_v, w2_v):
    B, H, S, d = q.shape
    G = NSA_GROUPS
    R = H // G
    f32 = jnp.float32
    scale = d ** -0.5
    pos = jnp.arange(S)
    kc = _compress(k_c, pe_k, w1_k, w2_k)
    vc = _compress(v_c, pe_v, w1_v, w2_v)
    n_cmp = kc.shape[2]
    qg = q.reshape(B, G, R, S, d)
    s_c = jnp.einsum('bgrsd,bgnd->bgrsn', qg, kc).astype(f32) * scale
    c_ok = (jnp.arange(n_cmp) * NSA_CMP_STRIDE + NSA_CMP_LEN - 1)[None, :] <= pos[:, None]
    p_c = jnp.where(c_ok, jax.nn.softmax(jnp.where(c_ok, s_c, NEG), axis=-1), 0.0)
    o_c = jnp.einsum('bgrsn,bgnd->bgrsd', p_c.astype(vc.dtype), vc)
    n_sel = S // NSA_SEL_LEN
    c_start = np.arange(n_cmp) * NSA_CMP_STRIDE
    s_start = np.arange(n_sel) * NSA_SEL_LEN
    cover = (c_start[:, None] < s_start[None, :] + NSA_SEL_LEN) & (c_start[:, None] + NSA_CMP_LEN > s_start[None, :])
    imp = jnp.einsum('bgrsn,nj->bgsj', p_c, jnp.asarray(cover, f32))
    blk = pos // NSA_SEL_LEN
    j = jnp.arange(n_sel)
    forced = (j[None, :] == 0) | (j[None, :] == blk[:, None]) | (j[None, :] == blk[:, None] - 1)
    imp = jnp.where(forced, FORCE, jnp.where(j[None, :] <= blk[:, None], imp, NEG))
    kk = min(NSA_TOPK, n_sel)
    s_val, s_idx = lax.top_k(imp, kk)
    s_ok = s_val > NEG / 2
    q_rot = _rope(q, positions, ROPE_DIM, ROPE_THETA).reshape(B, G, R, S, d)
    k_s = _rope(k_s, positions, ROPE_DIM, ROPE_THETA)
    k_w = _rope(k_w, positions, ROPE_DIM, ROPE_THETA)
    ksb = k_s.reshape(B, G, n_sel, NSA_SEL_LEN, d)
    vsb = v_s.reshape(B, G, n_sel, NSA_SEL_LEN, d)
    b_i = jnp.arange(B)[:, None, None, None]
    g_i = jnp.arange(G)[None, :, None, None]
    QC = NSA_QCHUNK

    def chunk(c):
        s0 = c * QC
        qc = lax.dynamic_slice_in_dim(q_rot, s0, QC, axis=3)
        idx = lax.dynamic_slice_in_dim(s_idx, s0, QC, axis=2)
        ok = lax.dynamic_slice_in_dim(s_ok, s0, QC, axis=2)
        k_sel = ksb[b_i, g_i, idx]
        v_sel = vsb[b_i, g_i, idx]
        sc = jnp.einsum('bgrqd,bgqnkd->bgrqnk', qc, k_sel).astype(f32) * scale
        k_pos = idx[..., None] * NSA_SEL_LEN + jnp.arange(NSA_SEL_LEN)
        q_pos = s0 + jnp.arange(QC)
        valid = ok[..., None] & (k_pos <= q_pos[:, None, None])
        sc = jnp.where(valid[:, :, None], sc, NEG).reshape(B, G, R, QC, kk * NSA_SEL_LEN)
        p = jax.nn.softmax(sc, axis=-1).astype(v_sel.dtype).reshape(B, G, R, QC, kk, NSA_SEL_LEN)
        return jnp.einsum('bgrqnk,bgqnkd->bgrqd', p, v_sel)

    o_s = lax.map(chunk, jnp.arange(S // QC))
    o_s = o_s.transpose(1, 2, 3, 0, 4, 5).reshape(B, G, R, S, d)
    nqb = S // WIN_QBLOCK
    span = NSA_WINDOW + WIN_QBLOCK
    kv_idx = jnp.arange(nqb)[:, None] * WIN_QBLOCK + jnp.arange(span)[None, :]
    padw = ((0, 0), (0, 0), (NSA_WINDOW, 0), (0, 0))
    k_win = jnp.pad(k_w, padw)[:, :, kv_idx]
    v_win = jnp.pad(v_w, padw)[:, :, kv_idx]
    qb = q_rot.reshape(B, G, R, nqb, WIN_QBLOCK, d)
    sw = jnp.einsum('bgrcqd,bgckd->bgrcqk', qb, k_win).astype(f32) * scale
    q_pos = pos.reshape(nqb, WIN_QBLOCK)
    k_pos = kv_idx - NSA_WINDOW
    dist = q_pos[:, :, None] - k_pos[:, None, :]
    w_ok = (dist >= 0) & (dist < NSA_WINDOW) & (k_pos[:, None, :] >= 0)
    p_w = jax.nn.softmax(jnp.where(w_ok, sw, NEG), axis=-1).astype(v_win.dtype)
    o_w = jnp.einsum('bgrcqk,bgckd->bgrcqd', p_w, v_win).reshape(B, G, R, S, d)
    gates = jax.nn.sigmoid(gate_logits.astype(f32)).reshape(B, S, H, 3).transpose(0, 2, 1, 3).reshape(B, G, R, S, 3)
    o = gates[..., 0:1] * o_c + gates[..., 1:2] * o_s + gates[..., 2:3] * o_w
    return o.reshape(B, H, S, d).astype(q.dtype)


def _mixer_ab(h, positions, w_in, pe_k, w1_k, w2_k, pe_v, w1_v, w2_v, w_out):
    mq, mk, mv, nq, nkc, nvc, nks, nvs, nkw, nvw, ng = _split(h @ w_in, AB_SIZES)
    mq = _rope(_heads(mq, MOBA_HEADS), positions, ROPE_DIM, ROPE_THETA)
    mk = _rope(_heads(mk, MOBA_HEADS), positions, ROPE_DIM, ROPE_THETA)
    o_a = _moba(mq, mk, _heads(mv, MOBA_HEADS))
    o_b = _nsa(_heads(nq, NSA_HEADS), _heads(nkc, NSA_GROUPS), _heads(nvc, NSA_GROUPS),
               _heads(nks, NSA_GROUPS), _heads(nvs, NSA_GROUPS), _heads(nkw, NSA_GROUPS),
               _heads(nvw, NSA_GROUPS), ng, positions, pe_k, w1_k, w2_k, pe_v, w1_v, w2_v)
    o = jnp.concatenate([o_a, o_b.astype(o_a.dtype)], axis=1)
    return _merge(o) @ w_out


def _retention(q, k, v):
    B, H, S, dk = q.shape
    dv = v.shape[-1]
    C = RET_CHUNK
    nch = S // C
    log_g = jnp.log(1.0 - jnp.exp2(-5.0 - jnp.arange(H, dtype=jnp.float32)))
    n = jnp.arange(C, dtype=jnp.float32)
    diff = n[:, None] - n[None, :]
    decay = jnp.where(diff >= 0, jnp.exp(jnp.maximum(diff, 0.0) * log_g[:, None, None]), 0.0)
    q_dec = jnp.exp((n + 1.0) * log_g[:, None])[:, :, None]
    k_dec = jnp.exp((C - 1.0 - n) * log_g[:, None])[:, :, None]
    c_dec = jnp.exp(C * log_g)[:, None, None]

    def to_chunks(t):
        return t.reshape(B, H, nch, C, t.shape[-1]).transpose(2, 0, 1, 3, 4)

    def step(state, xs):
        qi, ki, vi = xs
        intra = jnp.einsum('bhnm,bhmv->bhnv', jnp.einsum('bhnd,bhmd->bhnm', qi, ki) * decay, vi)
        cross = jnp.einsum('bhnd,bhdv->bhnv', qi, state) * q_dec
        state = state * c_dec + jnp.einsum('bhmd,bhmv->bhdv', ki * k_dec, vi)
        return state, intra + cross

    state0 = jnp.zeros((B, H, dk, dv), jnp.float32)
    _, out = lax.scan(step, state0, (to_chunks(q), to_chunks(k), to_chunks(v)))
    return out.transpose(1, 2, 0, 3, 4).reshape(B, H, S, dv)


def _mixer_c(h, positions, w_in, gn_gain, w_out):
    f32 = jnp.float32
    q, k, v, g = _split(h @ w_in, C_SIZES)
    q = _rope(_heads(q, RET_HEADS), positions, RET_DK, RET_THETA).astype(f32)
    k = _rope(_heads(k, RET_HEADS), positions, RET_DK, RET_THETA).astype(f32) * RET_DK ** -0.5
    v = _heads(v, RET_HEADS).astype(f32)
    y = _retention(q, k, v)
    mu = jnp.mean(y, axis=-1, keepdims=True)
    var = jnp.mean(jnp.square(y - mu), axis=-1, keepdims=True)
    y = (y - mu) * lax.rsqrt(var + RMS_EPS) * gn_gain.astype(f32)[None, :, None, :]
    y = _merge(y).astype(h.dtype)
    return (jax.nn.silu(g) * y) @ w_out


def _cross_attn(h, mem_n, w_q, w_kv, w_o):
    q = _heads(h @ w_q, X_HEADS)
    k, v = _split(mem_n @ w_kv, (X_HEADS * HEAD_DIM, X_HEADS * HEAD_DIM))
    k, v = _heads(k, X_HEADS), _heads(v, X_HEADS)
    s = jnp.einsum('bhsd,bhmd->bhsm', q, k).astype(jnp.float32) * HEAD_DIM ** -0.5
    p = jax.nn.softmax(s, axis=-1).astype(v.dtype)
    return _merge(jnp.einsum('bhsm,bhmd->bhsd', p, v)) @ w_o


def _conv_ffn(h, w_up, conv_w, conv_b, w_down):
    S = h.shape[1]
    u = h @ w_up
    up = jnp.pad(u, ((0, 0), (CONV_WIDTH - 1, 0), (0, 0)))
    c = conv_b + sum(conv_w[i] * up[:, i:i + S] for i in range(CONV_WIDTH))
    gate, val = _split(c, (D_FF, D_FF))
    return (jax.nn.silu(gate) * val) @ w_down


def setup_inputs(seed: int = 0) -> dict:
    key = jax.random.key(seed)
    ks = jax.random.split(key, 26)
    f32 = jnp.float32

    def nrm(k, shape, scale):
        return jax.random.normal(k, shape, f32) * scale

    def gain(k, shape):
        return 1.0 + nrm(k, shape, 0.02)

    D = D_MODEL
    positions = (jax.random.randint(ks[2], (BATCH, 1), 0, 1024, dtype=jnp.int32)
                 + jnp.arange(SEQ, dtype=jnp.int32)[None, :])
    return {
        'x': nrm(ks[0], (BATCH, SEQ, D), 1.0),
        'mem': nrm(ks[1], (BATCH, N_MEM, D), 1.0),
        'positions': positions,
        'norm_mix': gain(ks[3], (DEPTH, D)),
        'norm_cross': gain(ks[4], (DEPTH, D)),
        'norm_ffn': gain(ks[5], (DEPTH, D)),
        'norm_mem': gain(ks[6], (D,)),
        'norm_final': gain(ks[7], (D,)),
        'w_in_ab': nrm(ks[8], (N_EVEN, D, AB_COLS), D ** -0.5),
        'cmp_pe_k': nrm(ks[9], (N_EVEN, NSA_CMP_LEN, HEAD_DIM), 0.1),
        'cmp_w1_k': nrm(ks[10], (N_EVEN, NSA_CMP_LEN * HEAD_DIM, HEAD_DIM), (NSA_CMP_LEN * HEAD_DIM) ** -0.5),
        'cmp_w2_k': nrm(ks[11], (N_EVEN, HEAD_DIM, HEAD_DIM), HEAD_DIM ** -0.5),
        'cmp_pe_v': nrm(ks[12], (N_EVEN, NSA_CMP_LEN, HEAD_DIM), 0.1),
        'cmp_w1_v': nrm(ks[13], (N_EVEN, NSA_CMP_LEN * HEAD_DIM, HEAD_DIM), (NSA_CMP_LEN * HEAD_DIM) ** -0.5),
        'cmp_w2_v': nrm(ks[14], (N_EVEN, HEAD_DIM, HEAD_DIM), HEAD_DIM ** -0.5),
        'w_out_ab': nrm(ks[15], (N_EVEN, D, D), D ** -0.5),
        'w_in_c': nrm(ks[16], (N_ODD, D, C_COLS), D ** -0.5),
        'ret_gn': gain(ks[17], (N_ODD, RET_HEADS, RET_DV)),
        'w_out_c': nrm(ks[18], (N_ODD, RET_HEADS * RET_DV, D), (RET_HEADS * RET_DV) ** -0.5),
        'w_q_x': nrm(ks[19], (DEPTH, D, X_HEADS * HEAD_DIM), D ** -0.5),
        'w_kv_x': nrm(ks[20], (DEPTH, D, 2 * X_HEADS * HEAD_DIM), D ** -0.5),
        'w_o_x': nrm(ks[21], (DEPTH, X_HEADS * HEAD_DIM, D), (X_HEADS * HEAD_DIM) ** -0.5),
        'w_up': nrm(ks[22], (DEPTH, D, 2 * D_FF), D ** -0.5),
        'conv_w': nrm(ks[23], (DEPTH, CONV_WIDTH, 2 * D_FF), CONV_WIDTH ** -0.5),
        'conv_b': nrm(ks[24], (DEPTH, 2 * D_FF), 0.02),
        'w_down': nrm(ks[25], (DEPTH, D_FF, D), D_FF ** -0.5),
    }


def reference(x, mem, positions, norm_mix, norm_cross, norm_ffn, norm_mem, norm_final,
              w_in_ab, cmp_pe_k, cmp_w1_k, cmp_w2_k, cmp_pe_v, cmp_w1_v, cmp_w2_v, w_out_ab,
              w_in_c, ret_gn, w_out_c, w_q_x, w_kv_x, w_o_x, w_up, conv_w, conv_b, w_down):
    mem_n = _rmsnorm(mem, norm_mem)
    h = x
    for l in range(DEPTH):
        hn = _rmsnorm(h, norm_mix[l])
        if l % 2 == 0:
            e = l // 2
            h = h + _mixer_ab(hn, positions, w_in_ab[e], cmp_pe_k[e], cmp_w1_k[e], cmp_w2_k[e],
                              cmp_pe_v[e], cmp_w1_v[e], cmp_w2_v[e], w_out_ab[e])
        else:
            o = l // 2
            h = h + _mixer_c(hn, positions, w_in_c[o], ret_gn[o], w_out_c[o])
        h = h + _cross_attn(_rmsnorm(h, norm_cross[l]), mem_n, w_q_x[l], w_kv_x[l], w_o_x[l])
        h = h + _conv_ffn(_rmsnorm(h, norm_ffn[l]), w_up[l], conv_w[l], conv_b[l], w_down[l])
    return _rmsnorm(h, norm_final)
```

```python
import numpy as np
from contextlib import ExitStack
import concourse.bass as bass
import concourse.mybir as mybir
from concourse.bass_utils import run_bass_kernel_spmd

F32 = mybir.dt.float32; BF16 = mybir.dt.bfloat16; I32 = mybir.dt.int32
AF = mybir.ActivationFunctionType; ALU = mybir.AluOpType; AX = mybir.AxisListType

S = 4096; D = 2048; NMEM = 256; DEPTH = 4; DFF = 5632
EPS = 1e-6
NEGB = -30000.0


class Buf:
    def __init__(self, t, parent=None):
        self.t = t; self.w = {}; self.r = {}; self.parent = parent; self.kids = {}

    def sub(self, key):
        if key not in self.kids:
            self.kids[key] = Buf(self.t, self)
        return self.kids[key]

    def __getitem__(self, k):
        return self.t[k]

    def fam(self):
        out = [self]
        if self.parent is not None:
            out.append(self.parent)
        out.extend(self.kids.values())
        return out


class Ctx:
    NDS = {'sp': 24, 'pool': 12, 'act': 6}

    def __init__(self, nc):
        self.nc = nc; self.es = ExitStack()
        self.eng = {'pe': nc.tensor, 'act': nc.scalar, 'dve': nc.vector, 'pool': nc.gpsimd, 'sp': nc.sync}
        self.psem = {}; self.cnt = {}
        for e in ('pe', 'act', 'dve', 'pool'):
            self.psem[e] = self.es.enter_context(nc.semaphore("ps_" + e)); self.cnt[e] = 0
        self.dsem = {}; self.dtot = {}; self.dnext = {}
        for q, n in self.NDS.items():
            self.dsem[q] = [self.es.enter_context(nc.semaphore(f"ds_{q}{i}")) for i in range(n)]
            self.dtot[q] = [0] * n; self.dnext[q] = 0
        self.waited = {e: {} for e in self.eng}
        self.nuid = 0

    def uid(self, p="t"):
        self.nuid += 1
        return f"{p}{self.nuid}"

    def wait(self, en, sem, val):
        if val <= 0:
            return
        key = sem.name
        if self.waited[en].get(key, 0) >= val:
            return
        self.waited[en][key] = val
        self.eng[en].wait_ge(sem, val)

    def _deps(self, en, reads, writes):
        deps = []
        for b in reads:
            for f in b.fam():
                deps.extend(f.w.values())
        for b in writes:
            for f in b.fam():
                deps.extend(f.w.values()); deps.extend(f.r.values())
        for (s, v) in deps:
            if en == 'pe' and s is self.psem['pe']:
                continue
            self.wait(en, s, v)

    def _record(self, ev, reads, writes):
        k = ev[0].name
        for b in reads:
            if b.r.get(k, (None, 0))[1] < ev[1]:
                b.r[k] = ev
        for b in writes:
            b.w = {k: ev}; b.r = {}

    def op(self, en, emit, reads=(), writes=(), inc=True):
        self._deps(en, reads, writes)
        ins = emit(self.eng[en])
        if inc:
            self.cnt[en] += 1
            ins.then_inc(self.psem[en], 1)
            ev = (self.psem[en], self.cnt[en])
        else:
            ev = (self.psem[en], self.cnt[en] + 1)
        self._record(ev, reads, writes)
        return ins

    def dma(self, q, out, in_, reads=(), writes=()):
        self._deps(q, reads, writes)
        i = self.dnext[q]; self.dnext[q] = (i + 1) % len(self.dsem[q])
        sem = self.dsem[q][i]
        self.wait(q, sem, self.dtot[q][i])
        self.eng[q].dma_start(out=out, in_=in_).then_inc(sem, 16)
        self.dtot[q][i] += 16
        self._record((sem, self.dtot[q][i]), reads, writes)

    def barrier(self):
        for en in self.eng:
            for e2, s in self.psem.items():
                if e2 != en:
                    self.wait(en, s, self.cnt[e2])
            for q in self.dsem:
                for s, t in zip(self.dsem[q], self.dtot[q]):
                    self.wait(en, s, t)


class Prog:
    def __init__(self, nlayers=DEPTH, parts=("mix", "x", "ffn")):
        self.nlayers = nlayers; self.parts = parts
        nc = bass.Bass("TRN2", target_bir_lowering=False)
        self.nc = nc
        self.c = Ctx(nc)
        self.es = self.c.es
        self.es.enter_context(nc.allow_non_contiguous_dma(reason="small strided loads"))
        self.inp = {}
        self.scr = {}

    def din(self, name, shape, dt=F32):
        self.inp[name] = self.nc.dram_tensor(name, list(shape), dt, kind="ExternalInput").ap()
        return self.inp[name]

    def dscr(self, name, shape, dt):
        if name not in self.scr:
            self.scr[name] = self.nc.dram_tensor(name, list(shape), dt, kind=("ExternalOutput" if name in getattr(self, "dbg", ()) else "Internal")).ap()
        return self.scr[name]

    def sb(self, es, shape, dt=F32):
        return Buf(es.enter_context(self.nc.sbuf_tensor(self.c.uid("sb"), list(shape), dt)))

    def setup(self):
        nc, c = self.nc, self.c
        self.PS = [Buf(self.es.enter_context(nc.psum_tensor(f"psb{i}", [128, 512], F32))) for i in range(8)]
        self.psi = 0
        self.idf = self.sb(self.es, [128, 128]); self.idb = self.sb(self.es, [128, 128], BF16)
        self.onef = self.sb(self.es, [128, 128]); self.oneb = self.sb(self.es, [128, 128], BF16)
        self._pt = [self.sb(self.es, [128, 512], BF16) for _ in range(3)]
        self._pti = 0
        self._rden = [self.sb(self.es, [128, 512]) for _ in range(2)]
        self._rdi = 0
        c.dma('sp', self.idf[:], self.inp["c_ident"], writes=[self.idf])
        c.op('dve', lambda e: e.tensor_copy(out=self.idb[:], in_=self.idf[:]), reads=[self.idf], writes=[self.idb])
        c.op('dve', lambda e: e.memset(self.onef[:], 1.0), writes=[self.onef])
        c.op('dve', lambda e: e.memset(self.oneb[:], 1.0), writes=[self.oneb])

    def ps(self):
        p = self.PS[self.psi]; self.psi = (self.psi + 1) % 4
        return p

    def to_feature_major(self, src, dst, ntok):
        c = self.c
        with ExitStack() as es:
            xin = [self.sb(es, [128, D]) for _ in range(2)]
            xo = [self.sb(es, [128, 16, 128]) for _ in range(2)]
            for tt in range(ntok // 128):
                a = xin[tt % 2]; o = xo[tt % 2]
                c.dma('sp', a[:], src[tt * 128:(tt + 1) * 128, :], writes=[a])
                for q4 in range(4):
                    p = self.ps()
                    for j in range(4):
                        kc = q4 * 4 + j
                        c.op('pe', lambda e, p=p, j=j, kc=kc: e.transpose(p[:, j * 128:(j + 1) * 128], a[:, kc * 128:(kc + 1) * 128], self.idf[:]),
                             reads=[a, self.idf], writes=[p], inc=(j == 3))
                    c.op('act' if q4 % 2 else 'dve',
                         (lambda e, p=p, q4=q4: e.copy(out=o[:, q4 * 4:(q4 + 1) * 4, :], in_=p[:].rearrange("p (j t) -> p j t", j=4))) if q4 % 2 else
                         (lambda e, p=p, q4=q4: e.tensor_copy(out=o[:, q4 * 4:(q4 + 1) * 4, :], in_=p[:].rearrange("p (j t) -> p j t", j=4))),
                         reads=[p], writes=[o])
                c.dma('sp', dst.rearrange("(kc p) s -> p kc s", p=128)[:, :, tt * 128:(tt + 1) * 128], o[:], reads=[o])
        c.barrier()

    def norm_input(self, es, hT, gam, t0, NT, XT, st):
        c = self.c
        if 'gt' not in st:
            st['gt'] = self.sb(es, [128, 16]); st['hb'] = [self.sb(es, [128, 16, 256]) for _ in range(2)]
            st['sq'] = self.sb(es, [128, 16, 256]); st['rs'] = self.sb(es, [128, 256])
            c.dma('sp', st['gt'][:], gam.rearrange("(kc p) -> p kc", p=128), writes=[st['gt']])
        gt = st['gt']; sq = st['sq']; rs = st['rs']
        BW = min(256, NT)
        for blk in range(NT // BW):
            hb = st['hb'][blk % 2]
            c.dma('sp', hb[:, :, 0:BW], hT.rearrange("(kc p) s -> p kc s", p=128)[:, :, t0 + blk * BW:t0 + (blk + 1) * BW], writes=[hb])
            c.op('act', lambda e: e.activation(out=sq[:, :, 0:BW], in_=hb[:, :, 0:BW], func=AF.Square), reads=[hb], writes=[sq])
            p = self.ps()
            for kc in range(16):
                c.op('pe', lambda e, kc=kc: e.matmul(p[:, 0:BW], self.onef[:], sq[:, kc, 0:BW], start=(kc == 0), stop=(kc == 15)),
                     reads=[sq, self.onef], writes=[p], inc=(kc == 15))
            c.op('dve', lambda e: e.tensor_scalar(out=rs[:, 0:BW], in0=p[:, 0:BW], scalar1=1.0 / D, scalar2=EPS, op0=ALU.mult, op1=ALU.add), reads=[p], writes=[rs])
            c.op('act', lambda e: e.activation(out=rs[:, 0:BW], in_=rs[:, 0:BW], func=AF.Sqrt), reads=[rs], writes=[rs])
            c.op('dve', lambda e: e.reciprocal(out=rs[:, 0:BW], in_=rs[:, 0:BW]), reads=[rs], writes=[rs])
            for kc in range(16):
                en = 'dve'
                c.op(en, lambda e, kc=kc: e.scalar_tensor_tensor(out=XT[:, kc, blk * BW:(blk + 1) * BW], in0=hb[:, kc, 0:BW], scalar=gt[:, kc:kc + 1],
                                                                  in1=rs[:, 0:BW], op0=ALU.mult, op1=ALU.mult), reads=[hb, rs, gt], writes=[XT.sub((blk * BW) // 512)])

    def linear(self, W, KC, ntok, NT, groups, epi, norm=None, src=None, CGMAX=512):
        c = self.c
        Wv = W.rearrange("(kc p) c -> p kc c", p=128)
        with ExitStack() as es:
            XT = self.sb(es, [128, KC, NT], BF16)
            wts = [self.sb(es, [128, KC, CGMAX], BF16) for _ in range(2)]
            st = {}
            WC = W.shape[1]
            items = [(g, gi) for g in range(ntok // NT) for gi in range(len(groups))]

            def load_w(k):
                g, gi = items[k]
                grp = groups[gi]; wt = wts[k % 2]
                runs = []
                for i, cc in enumerate(grp):
                    if runs and runs[-1][1] + runs[-1][2] == cc:
                        runs[-1][2] += 1
                    else:
                        runs.append([i, cc, 1])
                for (i0, cc0, n) in runs:
                    wd = min((cc0 + n) * 128, WC) - cc0 * 128
                    c.dma('pool', wt[:, :, i0 * 128:i0 * 128 + wd], Wv[:, :, cc0 * 128:cc0 * 128 + wd], writes=[wt])
            load_w(0)
            for k, (g, gi) in enumerate(items):
                grp = groups[gi]; wt = wts[k % 2]
                if gi == 0:
                    if norm is not None:
                        self.norm_input(es, norm[0], norm[1], g * NT, NT, XT, st)
                    else:
                        for tb in range(NT // 512):
                            c.dma('sp', XT[:, :, tb * 512:(tb + 1) * 512], src.rearrange("(kc p) s -> p kc s", p=128)[:, :, g * NT + tb * 512:g * NT + (tb + 1) * 512], writes=[XT.sub(tb)])
                if k + 1 < len(items):
                    load_w(k + 1)
                for i, cc in enumerate(grp):
                    TB = min(512, NT)
                    for tb in range(NT // TB):
                        if hasattr(epi, 'pre'):
                            epi.pre(cc, g * NT + tb * TB, TB)
                        p = self.ps()
                        M = min(128, WC - cc * 128)
                        for kc in range(KC):
                            c.op('pe', lambda e: e.matmul(p[0:M, 0:TB], wt[:, kc, i * 128:i * 128 + M], XT[:, kc, tb * TB:(tb + 1) * TB],
                                                          start=(kc == 0), stop=(kc == KC - 1)),
                                 reads=[wt, XT.sub(tb)], writes=[p], inc=(kc == KC - 1))
                        epi(cc, g * NT + tb * TB, TB, p)
            if hasattr(epi, 'flush'):
                epi.flush()
        c.barrier()

    def epi_store(self, es, dst, dt, scale=None):
        c = self.c
        stg = [self.sb(es, [128, 512], dt) for _ in range(3)]
        k = [0]

        def epi(cc, t0, TB, p, row0=None):
            s = stg[k[0] % 3]; k[0] += 1
            if scale is None:
                c.op('act', lambda e: e.copy(out=s[:, 0:TB], in_=p[:, 0:TB]), reads=[p], writes=[s])
            else:
                c.op('act', lambda e: e.mul(out=s[:, 0:TB], in_=p[:, 0:TB], mul=scale), reads=[p], writes=[s])
            r0 = cc * 128 if row0 is None else row0
            c.dma('sp', dst[r0:r0 + 128, t0:t0 + TB], s[:, 0:TB], reads=[s])
        return epi

    def epi_residual(self, es, hT):
        c = self.c
        stg = [self.sb(es, [128, 512]) for _ in range(3)]
        k = [0]

        cur = {}

        def pre(cc, t0, TB):
            s = stg[k[0] % 3]; k[0] += 1
            cur[(cc, t0)] = s
            c.dma('sp', s[:, 0:TB], hT[cc * 128:(cc + 1) * 128, t0:t0 + TB], writes=[s])

        def epi(cc, t0, TB, p):
            s = cur.pop((cc, t0))
            c.op('dve', lambda e: e.tensor_tensor(out=s[:, 0:TB], in0=s[:, 0:TB], in1=p[:, 0:TB], op=ALU.add), reads=[p, s], writes=[s])
            c.dma('sp', hT[cc * 128:(cc + 1) * 128, t0:t0 + TB], s[:, 0:TB], reads=[s])
        epi.pre = pre
        return epi

    def attention(self, es, qT, kT_tiles, v_tiles, units, scale, bias_fn, out_fn, extra_fn=None, clamp=False):
        c = self.c
        nq = qT.t.shape[-1] // 512
        for j in range(nq):
            ulist = units(j)
            po = self.PS[4 + (self._rdi % 2)]; pd = self.PS[6 + (self._rdi % 2)]
            nu = len(ulist)

            def emit_scores(ui):
                i = ulist[ui]
                ka, kb, nk = kT_tiles(i)
                bl = bias_fn(i, j)
                p = self.ps()
                c.op('pe', lambda e: e.matmul(p[0:nk, :], ka, qT[:, j * 512:(j + 1) * 512], start=True, stop=(len(bl) == 0)),
                     reads=[qT] + kb, writes=[p], inc=(len(bl) == 0))
                for bi, (bl_l, bl_r, bb) in enumerate(bl):
                    c.op('pe', lambda e: e.matmul(p[0:nk, :], bl_l, bl_r, start=False, stop=(bi == len(bl) - 1)),
                         reads=bb, writes=[p], inc=(bi == len(bl) - 1))
                pt = self._pt[self._pti % 3]; self._pti += 1
                c.op('act', lambda e: e.activation(out=pt[0:nk, :], in_=p[0:nk, :], func=AF.Exp, scale=scale), reads=[p], writes=[pt])
                return pt, nk

            def emit_pv(ui, pt, nk):
                i = ulist[ui]
                va, vb = v_tiles(i)
                last = (ui == nu - 1)
                c.op('pe', lambda e: e.matmul(po[:, :], va, pt[0:nk, :], start=(ui == 0), stop=last), reads=[pt] + vb, writes=[po], inc=last)
                c.op('pe', lambda e: e.matmul(pd[:, :], self.oneb[0:nk, :], pt[0:nk, :], start=(ui == 0), stop=last), reads=[pt, self.oneb], writes=[pd], inc=last)
                if extra_fn is not None:
                    extra_fn(i, j, ui, nu, pt, nk)
            prev = None
            for ui in range(nu):
                cur = emit_scores(ui)
                if prev is not None:
                    emit_pv(ui - 1, *prev)
                prev = cur
            emit_pv(nu - 1, *prev)
            rd = self._rden[self._rdi % 2]; self._rdi += 1
            if clamp:
                c.op('dve', lambda e: e.tensor_scalar_max(out=rd[:], in0=pd[:], scalar1=1e-30), reads=[pd], writes=[rd])
                c.op('dve', lambda e: e.reciprocal(out=rd[:], in_=rd[:]), reads=[rd], writes=[rd])
            else:
                c.op('dve', lambda e: e.reciprocal(out=rd[:], in_=pd[:]), reads=[pd], writes=[rd])
            out_fn(j, po, rd)

    def load_tokmajor(self, vsrc_rows, ntok, vt, vf, co=0):
        c = self.c
        if True:
            c.dma('sp', vf[:, 0:ntok], vsrc_rows, writes=[vf])
            for t4 in range(ntok // 512):
                p = self.ps()
                for j in range(4):
                    tt = t4 * 4 + j
                    c.op('pe', lambda e, j=j, tt=tt: e.transpose(p[:, j * 128:(j + 1) * 128], vf[:, tt * 128:(tt + 1) * 128], self.idf[:]),
                         reads=[vf, self.idf], writes=[p], inc=(j == 3))
                c.op('dve', lambda e: e.tensor_copy(out=vt[:, t4 * 4:(t4 + 1) * 4, co:co + 128], in_=p[:].rearrange("p (j d) -> p j d", j=4)), reads=[p], writes=[vt])
            if ntok % 512:
                base = (ntok // 512) * 4
                n = (ntok % 512) // 128
                p = self.ps()
                for j in range(n):
                    tt = base + j
                    c.op('pe', lambda e, j=j, tt=tt: e.transpose(p[:, j * 128:(j + 1) * 128], vf[:, tt * 128:(tt + 1) * 128], self.idf[:]),
                         reads=[vf, self.idf], writes=[p], inc=(j == n - 1))
                c.op('dve', lambda e: e.tensor_copy(out=vt[:, base:base + n, co:co + 128], in_=p[:, 0:n * 128].rearrange("p (j d) -> p j d", j=n)), reads=[p], writes=[vt])

    def rope_tables(self, inv_ap, nP, cosD, sinD):
        c = self.c; I = self.inp
        with ExitStack() as es:
            inv = self.sb(es, [128, 1]); pi_ = self.sb(es, [128, 512], I32); pf = self.sb(es, [128, 512])
            ang = self.sb(es, [128, 512]); ki = self.sb(es, [128, 512], I32); kf = self.sb(es, [128, 512])
            r = self.sb(es, [128, 512]); m = self.sb(es, [128, 512]); o = [self.sb(es, [128, 512]) for _ in range(2)]
            c.dma('sp', inv[0:nP, :], inv_ap, writes=[inv])
            k = 0
            for tb in range(S // 512):
                c.dma('sp', pi_[0:nP, :], I["positions"][tb * 512:(tb + 1) * 512].partition_broadcast(nP), writes=[pi_])
                c.op('dve', lambda e: e.tensor_copy(out=pf[0:nP, :], in_=pi_[0:nP, :]), reads=[pi_], writes=[pf])
                for which, dst in ((0, sinD), (1, cosD)):
                    c.op('dve', lambda e: e.tensor_scalar(out=ang[0:nP, :], in0=pf[0:nP, :], scalar1=inv[0:nP, 0:1], scalar2=float(which * np.pi / 2), op0=ALU.mult, op1=ALU.add),
                         reads=[pf, inv], writes=[ang])
                    c.op('dve', lambda e: e.tensor_scalar(out=ki[0:nP, :], in0=ang[0:nP, :], scalar1=float(1 / (2 * np.pi)), scalar2=None, op0=ALU.mult), reads=[ang], writes=[ki])
                    c.op('dve', lambda e: e.tensor_copy(out=kf[0:nP, :], in_=ki[0:nP, :]), reads=[ki], writes=[kf])
                    c.op('dve', lambda e: e.scalar_tensor_tensor(out=r[0:nP, :], in0=kf[0:nP, :], scalar=float(-2 * np.pi), in1=ang[0:nP, :], op0=ALU.mult, op1=ALU.add),
                         reads=[kf, ang], writes=[r])
                    c.op('dve', lambda e: e.tensor_scalar(out=m[0:nP, :], in0=r[0:nP, :], scalar1=float(np.pi), scalar2=float(2 * np.pi), op0=ALU.is_gt, op1=ALU.mult), reads=[r], writes=[m])
                    c.op('dve', lambda e: e.tensor_tensor(out=r[0:nP, :], in0=r[0:nP, :], in1=m[0:nP, :], op=ALU.subtract), reads=[r, m], writes=[r])
                    c.op('dve', lambda e: e.tensor_scalar(out=r[0:nP, :], in0=r[0:nP, :], scalar1=3.14159, scalar2=-3.14159, op0=ALU.min, op1=ALU.max), reads=[r], writes=[r])
                    ob = o[k % 2]; k += 1
                    c.op('act', lambda e: e.activation(out=ob[0:nP, :], in_=r[0:nP, :], func=AF.Sin), reads=[r], writes=[ob])
                    c.dma('sp', dst[:, tb * 512:(tb + 1) * 512], ob[0:nP, :], reads=[ob])
        c.barrier()

    def mixer_c(self, l):
        c = self.c; I = self.inp; o = l // 2
        hT = self.scr["hT"]
        rqT = self.dscr("rqT", [2048, S], BF16); rkT = self.dscr("rkT", [2048, S], BF16)
        rvT = self.dscr("rvT", [4096, S], F32); rgT = self.dscr("rgT", [4096, S], BF16); zT = self.dscr("zT", [4096, S], BF16)
        cosR = self.scr["cosR"]; sinR = self.scr["sinR"]
        NT = 2048
        with ExitStack() as es:
            x1 = self.sb(es, [128, NT]); cs = self.sb(es, [128, NT]); sn = self.sb(es, [128, NT])
            ta = [self.sb(es, [128, 512]) for _ in range(2)]; tb_ = [self.sb(es, [128, 512]) for _ in range(2)]
            o1 = [self.sb(es, [128, 512], BF16) for _ in range(2)]; o2 = [self.sb(es, [128, 512], BF16) for _ in range(2)]
            ev = self.epi_store(es, rvT, F32)
            gs = [self.sb(es, [128, 512], BF16) for _ in range(2)]
            k = [0]; curg = [-1]

            def epi(cc, t0, TB, p):
                g = t0 // NT; tl = t0 % NT
                if cc < 32:
                    if curg[0] != g:
                        curg[0] = g
                        c.dma('sp', cs[:], cosR[:, g * NT:(g + 1) * NT], writes=[cs])
                        c.dma('sp', sn[:], sinR[:, g * NT:(g + 1) * NT], writes=[sn])
                    if cc % 2 == 0:
                        c.op('act', lambda e: e.copy(out=x1[:, tl:tl + 512], in_=p[:]), reads=[p], writes=[x1.sub(tl)])
                    else:
                        i = k[0] % 2; k[0] += 1
                        a = ta[i]; b = tb_[i]
                        dst = rqT if cc < 16 else rkT
                        r0 = (cc % 16 - 1) * 128
                        c.op('pool', lambda e: e.tensor_tensor(out=a[:], in0=x1[:, tl:tl + 512], in1=cs[:, tl:tl + 512], op=ALU.mult), reads=[x1.sub(tl), cs], writes=[a])
                        c.op('dve', lambda e: e.tensor_tensor(out=b[:], in0=p[:], in1=sn[:, tl:tl + 512], op=ALU.mult), reads=[p, sn], writes=[b])
                        c.op('dve', lambda e: e.tensor_tensor(out=o1[i][:], in0=a[:], in1=b[:], op=ALU.subtract), reads=[a, b], writes=[o1[i]])
                        c.dma('sp', dst[r0:r0 + 128, t0:t0 + 512], o1[i][:], reads=[o1[i]])
                        c.op('pool', lambda e: e.tensor_tensor(out=a[:], in0=x1[:, tl:tl + 512], in1=sn[:, tl:tl + 512], op=ALU.mult), reads=[x1.sub(tl), sn], writes=[a])
                        c.op('dve', lambda e: e.tensor_tensor(out=b[:], in0=p[:], in1=cs[:, tl:tl + 512], op=ALU.mult), reads=[p, cs], writes=[b])
                        c.op('dve', lambda e: e.tensor_tensor(out=o2[i][:], in0=a[:], in1=b[:], op=ALU.add), reads=[a, b], writes=[o2[i]])
                        c.dma('sp', dst[r0 + 128:r0 + 256, t0:t0 + 512], o2[i][:], reads=[o2[i]])
                elif cc < 64:
                    ev(cc, t0, TB, p, row0=(cc - 32) * 128)
                else:
                    i = k[0] % 2; k[0] += 1
                    c.op('act', lambda e: e.activation(out=gs[i][:], in_=p[:], func=AF.Silu), reads=[p], writes=[gs[i]])
                    c.dma('sp', rgT[(cc - 64) * 128:(cc - 63) * 128, t0:t0 + 512], gs[i][:], reads=[gs[i]])
            self.linear(I["w_in_c"][o], 16, S, NT, [[4 * i + j for j in range(4)] for i in range(24)], epi, norm=(hT, I["norm_mix"][l]))
        with ExitStack() as es:
            qT = self.sb(es, [128, 2, S], BF16); kT = self.sb(es, [128, 2, S], BF16); vt = self.sb(es, [128, 32, 512], BF16)
            gT = self.sb(es, [128, 4, S], BF16); vfr = self.sb(es, [128, S])
            DT = self.sb(es, [128, 128]); QD = self.sb(es, [128, 128]); KD = self.sb(es, [128, 8]); gain = self.sb(es, [128, 512])
            stf = self.sb(es, [128, 2, 512]); stb = self.sb(es, [128, 2, 512], BF16)
            ATm = [self.sb(es, [128, 128], BF16) for _ in range(2)]; qd = [self.sb(es, [128, 2, 128], BF16) for _ in range(2)]
            kd = [self.sb(es, [128, 256], BF16) for _ in range(2)]
            sm = [self.sb(es, [128, 8]) for _ in range(2)]; junk = self.sb(es, [128, 512])
            yn = [self.sb(es, [128, 512]) for _ in range(2)]
            zst = [self.sb(es, [128, 4, 512], BF16) for _ in range(2)]
            c.dma('sp', KD[:], I["c_kdec"], writes=[KD])
            for h in range(8):
                gam = 1.0 - 2.0 ** (-5.0 - h)
                cdec = float(gam ** 128)
                c.dma('sp', qT[:], rqT[h * 256:(h + 1) * 256, :].rearrange("(dc p) s -> p dc s", p=128), writes=[qT])
                c.dma('sp', kT[:], rkT[h * 256:(h + 1) * 256, :].rearrange("(dc p) s -> p dc s", p=128), writes=[kT])
                c.dma('sp', gT[:], rgT[h * 512:(h + 1) * 512, :].rearrange("(vc p) s -> p vc s", p=128), writes=[gT])
                c.dma('sp', DT[:], I["c_decT"][h], writes=[DT]); c.dma('sp', QD[:], I["c_qdec"][h], writes=[QD])
                c.dma('sp', gain[:, 0:4], I["ret_gn"][o, h].rearrange("(vc p) -> p vc", p=128), writes=[gain])
                for vc in range(4):
                    c.op('pool', lambda e: e.tensor_scalar(out=gT[:, vc, :], in0=gT[:, vc, :], scalar1=gain[:, vc:vc + 1], scalar2=None, op0=ALU.mult), reads=[gT, gain], writes=[gT])
                for vc in range(4):
                    self.load_tokmajor(rvT[h * 512 + vc * 128:h * 512 + (vc + 1) * 128, :], S, vt, vfr, co=vc * 128)
                c.op('dve', lambda e: e.memset(stf[:], 0.0), writes=[stf])
                c.op('dve', lambda e: e.memset(stb[:], 0.0), writes=[stb])
                def stA(ch):
                    n0 = ch * 128; i = ch % 2
                    pA = self.ps()
                    for dc in range(2):
                        c.op('pe', lambda e: e.matmul(pA[:, 0:128], kT[:, dc, n0:n0 + 128], qT[:, dc, n0:n0 + 128], start=(dc == 0), stop=(dc == 1)),
                             reads=[kT, qT], writes=[pA], inc=(dc == 1))
                    c.op('dve', lambda e: e.tensor_tensor(out=ATm[i][:], in0=pA[:, 0:128], in1=DT[:], op=ALU.mult), reads=[pA, DT], writes=[ATm[i]])
                    for dc in range(2):
                        c.op('pool', lambda e: e.tensor_tensor(out=qd[i][:, dc, :], in0=qT[:, dc, n0:n0 + 128], in1=QD[:], op=ALU.mult), reads=[qT, QD], writes=[qd[i]])
                    pK = self.ps()
                    for dc in range(2):
                        c.op('pe', lambda e: e.matmul(pK[:, dc * 128:(dc + 1) * 128], kT[:, dc, n0:n0 + 128], self.idb[:], start=True, stop=True),
                             reads=[kT, self.idb], writes=[pK], inc=(dc == 1))
                    c.op('act', lambda e: e.mul(out=kd[i][:], in_=pK[:, 0:256], mul=KD[:, h:h + 1]), reads=[pK, KD], writes=[kd[i]])

                def stB(ch):
                    i = ch % 2
                    pY = self.PS[4 + ch % 2]
                    c.op('pe', lambda e: e.matmul(pY[:], ATm[i][:], vt[:, ch, :], start=True, stop=False), reads=[ATm[i], vt], writes=[pY], inc=False)
                    for dc in range(2):
                        c.op('pe', lambda e: e.matmul(pY[:], qd[i][:, dc, :], stb[:, dc, :], start=False, stop=(dc == 1)), reads=[qd[i], stb], writes=[pY], inc=(dc == 1))
                    for dc in range(2):
                        pS = self.ps()
                        c.op('pe', lambda e: e.matmul(pS[:], kd[i][:, dc * 128:(dc + 1) * 128], vt[:, ch, :], start=True, stop=True), reads=[kd[i], vt], writes=[pS])
                        c.op('dve', lambda e: e.scalar_tensor_tensor(out=stf[:, dc, :], in0=stf[:, dc, :], scalar=cdec, in1=pS[:], op0=ALU.mult, op1=ALU.add),
                             reads=[stf, pS], writes=[stf])
                    c.op('act', lambda e: e.copy(out=stb[:], in_=stf[:]), reads=[stf], writes=[stb])
                    return pY

                def stCn(ch, pY):
                    i = ch % 2
                    s_ = sm[i]
                    c.op('dve', lambda e: e.reduce_sum(out=s_[:, 0:1], in_=pY[:], axis=AX.X), reads=[pY], writes=[s_])
                    c.op('act', lambda e: e.activation(out=junk[:], in_=pY[:], func=AF.Square, accum_out=s_[:, 1:2]), reads=[pY], writes=[junk, s_])
                    c.op('dve', lambda e: e.tensor_scalar(out=s_[:, 2:3], in0=s_[:, 0:1], scalar1=1.0 / 512, scalar2=None, op0=ALU.mult), reads=[s_], writes=[s_])
                    c.op('dve', lambda e: e.tensor_tensor(out=s_[:, 3:4], in0=s_[:, 2:3], in1=s_[:, 2:3], op=ALU.mult), reads=[s_], writes=[s_])
                    c.op('dve', lambda e: e.scalar_tensor_tensor(out=s_[:, 4:5], in0=s_[:, 1:2], scalar=1.0 / 512, in1=s_[:, 3:4], op0=ALU.mult, op1=ALU.subtract), reads=[s_], writes=[s_])
                    c.op('dve', lambda e: e.tensor_scalar(out=s_[:, 4:5], in0=s_[:, 4:5], scalar1=EPS, scalar2=None, op0=ALU.add), reads=[s_], writes=[s_])
                    c.op('act', lambda e: e.activation(out=s_[:, 5:6], in_=s_[:, 4:5], func=AF.Sqrt), reads=[s_], writes=[s_])
                    c.op('dve', lambda e: e.reciprocal(out=s_[:, 6:7], in_=s_[:, 5:6]), reads=[s_], writes=[s_])
                    y = yn[i]
                    c.op('dve', lambda e: e.tensor_scalar(out=y[:], in0=pY[:], scalar1=s_[:, 2:3], scalar2=s_[:, 6:7], op0=ALU.subtract, op1=ALU.mult), reads=[pY, s_], writes=[y])

                def stCt(ch):
                    n0 = ch * 128; y = yn[ch % 2]
                    pT = self.PS[6 + ch % 2]
                    for vc in range(4):
                        c.op('pe', lambda e: e.transpose(pT[:, vc * 128:(vc + 1) * 128], y[:, vc * 128:(vc + 1) * 128], self.idf[:]), reads=[y, self.idf], writes=[pT], inc=(vc == 3))
                    zs = zst[(ch // 4) % 2]
                    c.op('dve', lambda e: e.tensor_tensor(out=zs[:, :, (ch % 4) * 128:(ch % 4 + 1) * 128], in0=pT[:].rearrange("p (v n) -> p v n", v=4),
                                                          in1=gT[:, :, n0:n0 + 128], op=ALU.mult), reads=[pT, gT], writes=[zs])
                    if ch % 4 == 3:
                        t0 = (ch // 4) * 512
                        c.dma('sp', zT[h * 512:(h + 1) * 512, t0:t0 + 512].rearrange("(vc p) s -> p vc s", p=128), zs[:], reads=[zs])

                stA(0)
                for ch in range(32):
                    if ch + 1 < 32:
                        stA(ch + 1)
                    pY_ = stB(ch)
                    stCn(ch, pY_)
                    if ch >= 1:
                        stCt(ch - 1)
                stCt(31)
        c.barrier()
        with ExitStack() as es:
            self.linear(I["w_out_c"][o], 32, S, 1024, [[4 * i + j for j in range(4)] for i in range(4)], self.epi_residual(es, hT), src=zT)

    def mixer_ab(self, l):
        c = self.c; I = self.inp; e_ = l // 2
        hT = self.scr["hT"]
        SC = 128 ** -0.5
        mqT = self.dscr("mqT", [1024, S], BF16); mkT = self.dscr("mkT", [1024, S], BF16); mvT = self.dscr("mvT", [1024, S], F32)
        nqT = self.dscr("nqT", [1024, S], BF16); nqrT = self.dscr("nqrT", [1024, S], BF16)
        nkcT = self.dscr("nkcT", [256, S], BF16); nvcT = self.dscr("nvcT", [256, S], BF16)
        nksT = self.dscr("nksT", [256, S], BF16); nvsT = self.dscr("nvsT", [256, S], F32)
        nkwT = self.dscr("nkwT", [256, S], BF16); nvwT = self.dscr("nvwT", [256, S], F32)
        ngT = self.dscr("ngT", [24, S], F32); oT = self.dscr("oT", [2048, S], BF16); ocT = self.dscr("ocT", [1024, S], F32)
        cos32 = self.scr["cos32"]; sin32 = self.scr["sin32"]
        NT = 1024
        with ExitStack() as es:
            cs = self.sb(es, [32, NT]); sn = self.sb(es, [32, NT]); pm = self.sb(es, [32, 32])
            xs = [self.sb(es, [128, 512]) for _ in range(2)]
            ta = [self.sb(es, [32, 512]) for _ in range(2)]; tb_ = [self.sb(es, [32, 512]) for _ in range(2)]
            ob = [self.sb(es, [128, 512], BF16) for _ in range(2)]
            gsg = [self.sb(es, [24, 512]) for _ in range(2)]
            c.dma('sp', pm[:], I["c_perm"], writes=[pm])
            e_bf = {}
            for nm, dst in (("mv", mvT), ("nvs", nvsT), ("nvw", nvwT)):
                e_bf[nm] = self.epi_store(es, dst, F32)
            for nm, dst in (("nq", nqT), ("nkc", nkcT), ("nvc", nvcT)):
                e_bf[nm] = self.epi_store(es, dst, BF16)
            k = [0]; curg = [-1]

            def rope_store(cc, t0, p, dst, r0):
                g = t0 // NT; tl = t0 % NT
                if curg[0] != g:
                    curg[0] = g
                    c.dma('sp', cs[:], cos32[:, g * NT:(g + 1) * NT], writes=[cs])
                    c.dma('sp', sn[:], sin32[:, g * NT:(g + 1) * NT], writes=[sn])
                i = k[0] % 2; k[0] += 1
                x = xs[i]; a = ta[i]; b = tb_[i]; o = ob[i]
                c.op('act', lambda e: e.copy(out=x[:], in_=p[:]), reads=[p], writes=[x])
                pp = self.ps()
                c.op('pe', lambda e: e.matmul(pp[0:32, :], pm[:], x[0:32, :], start=True, stop=True), reads=[pm, x], writes=[pp])
                c.op('pool', lambda e: e.tensor_tensor(out=a[:], in0=x[0:32, :], in1=cs[:, tl:tl + 512], op=ALU.mult), reads=[x, cs], writes=[a])
                c.op('dve', lambda e: e.tensor_tensor(out=b[:], in0=pp[0:32, :], in1=sn[:, tl:tl + 512], op=ALU.mult), reads=[pp, sn], writes=[b])
                c.op('act', lambda e: e.copy(out=o[:], in_=x[:]), reads=[x], writes=[o])
                c.op('dve', lambda e: e.tensor_tensor(out=o[0:32, :], in0=a[:], in1=b[:], op=ALU.add), reads=[a, b], writes=[o])
                c.dma('sp', dst[r0:r0 + 128, t0:t0 + 512], o[:], reads=[o])

            def epi(cc, t0, TB, p):
                if cc < 8:
                    rope_store(cc, t0, p, mqT, cc * 128)
                elif cc < 16:
                    rope_store(cc, t0, p, mkT, (cc - 8) * 128)
                elif cc < 24:
                    e_bf["mv"](cc, t0, TB, p, row0=(cc - 16) * 128)
                elif cc < 32:
                    e_bf["nq"](cc, t0, TB, p, row0=(cc - 24) * 128)
                    rope_store(cc, t0, p, nqrT, (cc - 24) * 128)
                elif cc < 34:
                    e_bf["nkc"](cc, t0, TB, p, row0=(cc - 32) * 128)
                elif cc < 36:
                    e_bf["nvc"](cc, t0, TB, p, row0=(cc - 34) * 128)
                elif cc < 38:
                    rope_store(cc, t0, p, nksT, (cc - 36) * 128)
                elif cc < 40:
                    e_bf["nvs"](cc, t0, TB, p, row0=(cc - 38) * 128)
                elif cc < 42:
                    rope_store(cc, t0, p, nkwT, (cc - 40) * 128)
                elif cc < 44:
                    e_bf["nvw"](cc, t0, TB, p, row0=(cc - 42) * 128)
                else:
                    i = k[0] % 2; k[0] += 1
                    c.op('act', lambda e: e.activation(out=gsg[i][:], in_=p[0:24, :], func=AF.Sigmoid), reads=[p], writes=[gsg[i]])
                    c.dma('sp', ngT[:, t0:t0 + 512], gsg[i][:], reads=[gsg[i]])
            groups = [[4 * i + j for j in range(4)] for i in range(11)] + [[44]]
            self.linear(I["w_in_ab"][e_], 16, S, NT, groups, epi, norm=(hT, I["norm_mix"][l]))
        with ExitStack() as es:
            qT = self.sb(es, [128, S], BF16); kT = self.sb(es, [128, S], BF16); vt = self.sb(es, [128, 32, 128], BF16); vf = self.sb(es, [128, S])
            E16 = self.sb(es, [16, S], BF16); caus = self.sb(es, [128, 4, 512], BF16); selbT = self.sb(es, [16, S], BF16)
            valid = self.sb(es, [128, 512]); own = self.sb(es, [128, 512])
            km = self.sb(es, [128, 16]); kmb = self.sb(es, [128, 16], BF16)
            gm = self.sb(es, [128, 32, 16]); t8 = self.sb(es, [128, 32, 8]); thr = self.sb(es, [128, 32, 1]); selm = self.sb(es, [128, 32, 16])
            ost = [self.sb(es, [128, 512], BF16) for _ in range(2)]
            c.dma('pool', E16[:], I["c_E16"], writes=[E16])
            c.dma('pool', caus[:], I["c_causal"].rearrange("m k q -> k m q"), writes=[caus])
            c.dma('sp', valid[:], I["c_mobavalid"], writes=[valid]); c.dma('sp', own[:], I["c_mobaown"], writes=[own])
            kk = [0]
            for h in range(8):
                c.dma('sp', qT[:], mqT[h * 128:(h + 1) * 128, :], writes=[qT])
                c.dma('sp', kT[:], mkT[h * 128:(h + 1) * 128, :], writes=[kT])
                self.load_tokmajor(mvT[h * 128:(h + 1) * 128, :], S, vt, vf)
                c.op('dve', lambda e: e.tensor_reduce(out=km[:], in_=kT[:].rearrange("p (n k) -> p n k", k=256), axis=AX.X, op=ALU.add), reads=[kT], writes=[km])
                c.op('dve', lambda e: e.tensor_scalar(out=kmb[:], in0=km[:], scalar1=1.0 / 256, scalar2=None, op0=ALU.mult), reads=[km], writes=[kmb])
                pg = self.ps()
                for qt in range(32):
                    c.op('pe', lambda e: e.matmul(pg[:, qt * 16:(qt + 1) * 16], qT[:, qt * 128:(qt + 1) * 128], kmb[:], start=True, stop=True),
                         reads=[qT, kmb], writes=[pg], inc=(qt == 31))
                c.op('dve', lambda e: e.tensor_tensor(out=gm[:].rearrange("p a b -> p (a b)"), in0=pg[:], in1=valid[:], op=ALU.add), reads=[pg, valid], writes=[gm])
                for qt in range(32):
                    c.op('dve', lambda e: e.max(out=t8[:, qt, :], in_=gm[:, qt, :]), reads=[gm], writes=[t8])
                c.op('dve', lambda e: e.tensor_scalar_max(out=thr[:], in0=t8[:, :, 2:3], scalar1=-1e29), reads=[t8], writes=[thr])
                c.op('dve', lambda e: e.tensor_tensor(out=selm[:], in0=gm[:], in1=thr[:].to_broadcast([128, 32, 16]), op=ALU.is_ge), reads=[gm, thr], writes=[selm])
                c.op('dve', lambda e: e.tensor_tensor(out=selm[:].rearrange("p a b -> p (a b)"), in0=selm[:].rearrange("p a b -> p (a b)"), in1=own[:], op=ALU.max), reads=[selm, own], writes=[selm])
                c.op('dve', lambda e: e.tensor_scalar(out=selm[:], in0=selm[:], scalar1=-1.0, scalar2=-NEGB, op0=ALU.add, op1=ALU.mult), reads=[selm], writes=[selm])
                for q4 in range(8):
                    pT = self.ps()
                    for j in range(4):
                        qt = q4 * 4 + j
                        c.op('pe', lambda e: e.transpose(pT[0:16, j * 128:(j + 1) * 128], selm[:, qt, :], self.idf[:]), reads=[selm, self.idf], writes=[pT], inc=(j == 3))
                    c.op('act', lambda e: e.copy(out=selbT[:, q4 * 512:(q4 + 1) * 512], in_=pT[0:16, :]), reads=[pT], writes=[selbT.sub(q4)])

                def bias_fn(i, j):
                    bl = [(E16[:, i * 128:(i + 1) * 128], selbT[:, j * 512:(j + 1) * 512], [E16, selbT.sub(j)])]
                    if i >= 4 * j:
                        bl.append((self.idb[:], caus[:, i - 4 * j, :], [self.idb, caus]))
                    return bl

                def out_fn(j, po, rd, h=h):
                    o = ost[kk[0] % 2]; kk[0] += 1
                    c.op('dve', lambda e: e.tensor_tensor(out=o[:], in0=po[:], in1=rd[:], op=ALU.mult), reads=[po, rd], writes=[o])
                    c.dma('sp', oT[h * 128:(h + 1) * 128, j * 512:(j + 1) * 512], o[:], reads=[o])
                self.attention(es, qT, lambda i: (kT[:, i * 128:(i + 1) * 128], [kT], 128), lambda i: (vt[:, i, :], [vt]),
                               lambda j: list(range(4 * j + 4)), SC, bias_fn, out_fn)
        c.barrier()
        for g in range(2):
            with ExitStack() as es:
                kcc = self.sb(es, [128, 256], BF16); vcc = self.sb(es, [128, 2, 128], BF16)
                with ExitStack() as es2:
                    src = self.sb(es2, [128, S], BF16); kA = self.sb(es2, [128, S], BF16); kB = self.sb(es2, [128, S], BF16)
                    pe = self.sb(es2, [128, 32]); w1 = self.sb(es2, [128, 32, 128], BF16); w2 = self.sb(es2, [128, 128], BF16)
                    hc = self.sb(es2, [128, 256], BF16)
                    for which in range(2):
                        srcD = (nkcT, nvcT)[which]; peD = (I["cmp_pe_k"], I["cmp_pe_v"])[which][e_]
                        w1D = (I["cmp_w1_k"], I["cmp_w1_v"])[which][e_]; w2D = (I["cmp_w2_k"], I["cmp_w2_v"])[which][e_]
                        c.dma('sp', src[:], srcD[g * 128:(g + 1) * 128, :], writes=[src])
                        c.dma('sp', pe[:], peD.rearrange("t d -> d t"), writes=[pe])
                        c.dma('pool', w1[:], w1D.rearrange("(t d) j -> d t j", d=128), writes=[w1])
                        c.dma('pool', w2[:], w2D, writes=[w2])
                        sv = src[:].rearrange("p (m r) -> p m r", r=16)
                        c.op('dve', lambda e: e.tensor_tensor(out=kA[:].rearrange("p (m r) -> p m r", r=16), in0=sv, in1=pe[:, 0:16].unsqueeze(1).to_broadcast([128, 256, 16]), op=ALU.add),
                             reads=[src, pe], writes=[kA])
                        c.op('dve', lambda e: e.tensor_tensor(out=kB[:].rearrange("p (m r) -> p m r", r=16), in0=sv, in1=pe[:, 16:32].unsqueeze(1).to_broadcast([128, 256, 16]), op=ALU.add),
                             reads=[src, pe], writes=[kB])
                        ph = self.ps()
                        kAv = kA[:].rearrange("p (m r) -> p m r", r=16); kBv = kB[:].rearrange("p (m r) -> p m r", r=16)
                        for t in range(32):
                            rhs = kAv[:, 0:255, t] if t < 16 else kBv[:, 1:256, t - 16]
                            c.op('pe', lambda e: e.matmul(ph[:, 0:255], w1[:, t, :], rhs, start=(t == 0), stop=(t == 31)), reads=[w1, kA, kB], writes=[ph], inc=(t == 31))
                        c.op('dve', lambda e: e.memset(hc[:], 0.0), writes=[hc])
                        c.op('act', lambda e: e.activation(out=hc[:, 0:255], in_=ph[:, 0:255], func=AF.Silu), reads=[ph], writes=[hc])
                        if which == 0:
                            pk = self.ps()
                            c.op('pe', lambda e: e.matmul(pk[:, 0:256], w2[:], hc[:], start=True, stop=True), reads=[w2, hc], writes=[pk])
                            c.op('act', lambda e: e.copy(out=kcc[:], in_=pk[:, 0:256]), reads=[pk], writes=[kcc])
                        else:
                            pv = self.ps()
                            for nt in range(2):
                                c.op('pe', lambda e: e.matmul(pv[:, nt * 128:(nt + 1) * 128], hc[:, nt * 128:(nt + 1) * 128], w2[:], start=True, stop=True), reads=[w2, hc], writes=[pv], inc=(nt == 1))
                            c.op('act', lambda e: e.copy(out=vcc[:].rearrange("p a b -> p (a b)"), in_=pv[:, 0:256]), reads=[pv], writes=[vcc])
                c.barrier()
                qT = self.sb(es, [128, S], BF16); kT = self.sb(es, [128, S], BF16); vt = self.sb(es, [128, 32, 128], BF16); vf = self.sb(es, [128, S])
                vt2 = self.sb(es, [128, 32, 128], BF16); kT2 = self.sb(es, [128, S], BF16)
                cmpb = self.sb(es, [128, 2, S], BF16); cover = self.sb(es, [128, 2, 64], BF16); winb = self.sb(es, [128, 8, 512], BF16)
                caus = self.sb(es, [128, 4, 512], BF16); E64 = self.sb(es, [64, S], BF16); selbT = self.sb(es, [64, S], BF16)
                impacc = self.sb(es, [64, S]); ngs = self.sb(es, [24, S]); gsel = self.sb(es, [24, 3072])
                keep = self.sb(es, [128, 32, 64]); addm = self.sb(es, [128, 32, 64])
                hacc = self.sb(es, [128, S])
                sc_ = [self.sb(es, [128, 512]) for _ in range(2)]; tmp = [self.sb(es, [128, 512]) for _ in range(2)]
                ocb = [self.sb(es, [128, 512]) for _ in range(2)]; ost = [self.sb(es, [128, 512], BF16) for _ in range(2)]
                iv = self.sb(es, [128, 8, 64]); iv2 = self.sb(es, [128, 64]); t8a = self.sb(es, [128, 8]); t8b = self.sb(es, [128, 8]); th = self.sb(es, [128, 1])
                sb_ = self.sb(es, [128, 8, 64])
                c.dma('pool', cmpb[:], I["c_cmpbias"].rearrange("i n q -> n i q"), writes=[cmpb])
                c.dma('pool', cover[:], I["c_cover"].rearrange("i n j -> n i j"), writes=[cover])
                c.dma('pool', winb[:], I["c_winbias"].rearrange("m k q -> k m q"), writes=[winb])
                c.dma('pool', caus[:], I["c_causal"].rearrange("m k q -> k m q"), writes=[caus])
                c.dma('pool', E64[:], I["c_E64"], writes=[E64])
                c.dma('sp', ngs[:], ngT, writes=[ngs]); c.dma('sp', gsel[:], I["c_gsel"], writes=[gsel])
                c.dma('sp', keep[:], I["c_selkeep"], writes=[keep]); c.dma('sp', addm[:], I["c_seladd"], writes=[addm])
                kk = [0]

                def gate_scale(hh, br, j, rd):
                    col = hh * 3 + br
                    pgt = self.ps()
                    c.op('pe', lambda e: e.matmul(pgt[:], gsel[:, col * 128:(col + 1) * 128], ngs[:, j * 512:(j + 1) * 512], start=True, stop=True), reads=[gsel, ngs], writes=[pgt])
                    s_ = sc_[kk[0] % 2]
                    c.op('dve', lambda e: e.tensor_tensor(out=s_[:], in0=pgt[:], in1=rd[:], op=ALU.mult), reads=[pgt, rd], writes=[s_])
                    return s_

                for r in range(4):
                    hh = g * 4 + r
                    c.dma('sp', qT[:], nqT[hh * 128:(hh + 1) * 128, :], writes=[qT])
                    st = {}

                    def extra_fn(i, j, ui, nu, pt, nk):
                        if ui == 0:
                            st['pimp'] = self.ps()
                        c.op('pe', lambda e: e.matmul(st['pimp'][0:64, :], cover[:, i, :], pt[:], start=(ui == 0), stop=(ui == nu - 1)), reads=[cover, pt], writes=[st['pimp']], inc=(ui == nu - 1))

                    def out_fn(j, po, rd, r=r, hh=hh):
                        pimp = st['pimp']
                        jb = impacc.sub(j)
                        if r == 0:
                            c.op('dve', lambda e: e.tensor_tensor(out=impacc[:, j * 512:(j + 1) * 512], in0=pimp[0:64, :], in1=rd[0:64, :], op=ALU.mult), reads=[pimp, rd], writes=[jb])
                        else:
                            t_ = tmp[kk[0] % 2]
                            c.op('dve', lambda e: e.tensor_tensor(out=t_[0:64, :], in0=pimp[0:64, :], in1=rd[0:64, :], op=ALU.mult), reads=[pimp, rd], writes=[t_])
                            c.op('pool', lambda e: e.tensor_tensor(out=impacc[:, j * 512:(j + 1) * 512], in0=impacc[:, j * 512:(j + 1) * 512], in1=t_[0:64, :], op=ALU.add), reads=[t_, jb], writes=[jb])
                        s_ = gate_scale(hh, 0, j, rd)
                        o = ocb[kk[0] % 2]; kk[0] += 1
                        c.op('dve', lambda e: e.tensor_tensor(out=o[:], in0=po[:], in1=s_[:], op=ALU.mult), reads=[po, s_], writes=[o])
                        c.dma('sp', ocT[hh * 128:(hh + 1) * 128, j * 512:(j + 1) * 512], o[:], reads=[o])
                    self.attention(es, qT, lambda i: (kcc[:, i * 128:(i + 1) * 128], [kcc], 128), lambda i: (vcc[:, i, :], [vcc]),
                                   lambda j: [0] if j < 4 else [0, 1], SC, lambda i, j: [(self.idb[:], cmpb[:, i, j * 512:(j + 1) * 512], [self.idb, cmpb])],
                                   out_fn, extra_fn=extra_fn, clamp=True)
                for q8 in range(4):
                    pI = self.ps()
                    for j in range(8):
                        qt = q8 * 8 + j
                        c.op('pe', lambda e: e.transpose(pI[:, j * 64:(j + 1) * 64], impacc[:, qt * 128:(qt + 1) * 128], self.idf[0:64, 0:64]), reads=[impacc, self.idf], writes=[pI], inc=(j == 7))
                    c.op('dve', lambda e: e.tensor_tensor(out=iv[:], in0=pI[:].rearrange("p (a b) -> p a b", b=64), in1=keep[:, q8 * 8:(q8 + 1) * 8, :], op=ALU.mult), reads=[pI, keep], writes=[iv])
                    c.op('dve', lambda e: e.tensor_tensor(out=iv[:], in0=iv[:], in1=addm[:, q8 * 8:(q8 + 1) * 8, :], op=ALU.add), reads=[iv, addm], writes=[iv])
                    for j in range(8):
                        c.op('dve', lambda e: e.max(out=t8a[:], in_=iv[:, j, :]), reads=[iv], writes=[t8a])
                        c.op('dve', lambda e: e.match_replace(out=iv2[:], in_to_replace=t8a[:], in_values=iv[:, j, :], imm_value=-3e38), reads=[iv, t8a], writes=[iv2])
                        c.op('dve', lambda e: e.max(out=t8b[:], in_=iv2[:]), reads=[iv2], writes=[t8b])
                        c.op('dve', lambda e: e.tensor_scalar_max(out=th[:], in0=t8b[:, 7:8], scalar1=-1e29), reads=[t8b], writes=[th])
                        c.op('dve', lambda e: e.tensor_scalar(out=sb_[:, j, :], in0=iv[:, j, :], scalar1=th[:, 0:1], scalar2=None, op0=ALU.is_ge), reads=[iv, th], writes=[sb_])
                    c.op('dve', lambda e: e.tensor_scalar(out=sb_[:], in0=sb_[:], scalar1=-1.0, scalar2=-NEGB, op0=ALU.add, op1=ALU.mult), reads=[sb_], writes=[sb_])
                    for j2 in range(2):
                        pT = self.ps()
                        for j in range(4):
                            c.op('pe', lambda e: e.transpose(pT[0:64, j * 128:(j + 1) * 128], sb_[:, j2 * 4 + j, :], self.idf[:]), reads=[sb_, self.idf], writes=[pT], inc=(j == 3))
                        q4 = q8 * 2 + j2
                        c.op('act', lambda e: e.copy(out=selbT[:, q4 * 512:(q4 + 1) * 512], in_=pT[0:64, :]), reads=[pT], writes=[selbT.sub(q4)])
                c.dma('sp', kT[:], nksT[g * 128:(g + 1) * 128, :], writes=[kT])
                self.load_tokmajor(nvsT[g * 128:(g + 1) * 128, :], S, vt, vf)
                c.dma('sp', kT2[:], nkwT[g * 128:(g + 1) * 128, :], writes=[kT2])
                self.load_tokmajor(nvwT[g * 128:(g + 1) * 128, :], S, vt2, vf)
                for r in range(4):
                    hh = g * 4 + r
                    c.dma('sp', qT[:], nqrT[hh * 128:(hh + 1) * 128, :], writes=[qT])

                    def bias_sel(i, j):
                        bl = [(E64[:, i * 128:(i + 1) * 128], selbT[:, j * 512:(j + 1) * 512], [E64, selbT.sub(j)])]
                        if i >= 4 * j:
                            bl.append((self.idb[:], caus[:, i - 4 * j, :], [self.idb, caus]))
                        return bl

                    def out_sel(j, po, rd, hh=hh):
                        s_ = gate_scale(hh, 1, j, rd); kk[0] += 1
                        c.op('dve', lambda e: e.tensor_tensor(out=hacc[:, j * 512:(j + 1) * 512], in0=po[:], in1=s_[:], op=ALU.mult), reads=[po, s_], writes=[hacc.sub(j)])
                    self.attention(es, qT, lambda i: (kT[:, i * 128:(i + 1) * 128], [kT], 128), lambda i: (vt[:, i, :], [vt]),
                                   lambda j: list(range(4 * j + 4)), SC, bias_sel, out_sel)

                    def out_win(j, po, rd, hh=hh):
                        s_ = gate_scale(hh, 2, j, rd)
                        t_ = tmp[kk[0] % 2]; o = ost[kk[0] % 2]; ob_ = ocb[kk[0] % 2]; kk[0] += 1
                        c.dma('sp', ob_[:], ocT[hh * 128:(hh + 1) * 128, j * 512:(j + 1) * 512], writes=[ob_])
                        c.op('dve', lambda e: e.tensor_tensor(out=t_[:], in0=po[:], in1=s_[:], op=ALU.mult), reads=[po, s_], writes=[t_])
                        c.op('pool', lambda e: e.tensor_tensor(out=t_[:], in0=t_[:], in1=hacc[:, j * 512:(j + 1) * 512], op=ALU.add), reads=[t_, hacc.sub(j)], writes=[t_])
                        c.op('pool', lambda e: e.tensor_tensor(out=o[:], in0=t_[:], in1=ob_[:], op=ALU.add), reads=[t_, ob_], writes=[o])
                        c.dma('sp', oT[1024 + hh * 128:1024 + (hh + 1) * 128, j * 512:(j + 1) * 512], o[:], reads=[o])
                    self.attention(es, qT, lambda i: (kT2[:, i * 128:(i + 1) * 128], [kT2], 128), lambda i: (vt2[:, i, :], [vt2]),
                                   lambda j: list(range(max(0, 4 * j - 4), 4 * j + 4)), SC,
                                   lambda i, j: [(self.idb[:], winb[:, i - 4 * j + 4, :], [self.idb, winb])], out_win)
            c.barrier()
        with ExitStack() as es:
            self.linear(I["w_out_ab"][e_], 16, S, 2048, [[4 * i + j for j in range(4)] for i in range(4)], self.epi_residual(es, hT), src=oT)

    def cross_attn(self, l):
        c = self.c; I = self.inp
        hT = self.scr["hT"]
        xqT = self.dscr("xqT", [512, S], BF16); kxT = self.dscr("kxT", [512, NMEM], BF16)
        vxT = self.dscr("vxT", [512, NMEM], F32); oxT = self.dscr("oxT", [512, S], BF16)
        with ExitStack() as es:
            self.linear(I["w_q_x"][l], 16, S, 2048, [[0, 1, 2, 3]], self.epi_store(es, xqT, BF16), norm=(hT, I["norm_cross"][l]))
        with ExitStack() as es:
            ek = self.epi_store(es, kxT, BF16); ev = self.epi_store(es, vxT, F32)

            def epi(cc, t0, TB, p):
                if cc < 4:
                    ek(cc, t0, TB, p)
                else:
                    ev(cc, t0, TB, p, row0=(cc - 4) * 128)
            self.linear(I["w_kv_x"][l], 16, NMEM, NMEM, [[0, 1, 2, 3], [4, 5, 6, 7]], epi, norm=(self.scr["memT"], I["norm_mem"]))
        with ExitStack() as es:
            qT = self.sb(es, [128, S], BF16); kT = self.sb(es, [128, NMEM], BF16); vt = self.sb(es, [128, 2, 128], BF16); vfx = self.sb(es, [128, NMEM])
            ost = [self.sb(es, [128, 512], BF16) for _ in range(2)]
            k = [0]
            for h in range(4):
                c.dma('sp', qT[:], xqT[h * 128:(h + 1) * 128, :], writes=[qT])
                c.dma('sp', kT[:], kxT[h * 128:(h + 1) * 128, :], writes=[kT])
                self.load_tokmajor(vxT[h * 128:(h + 1) * 128, :], NMEM, vt, vfx)

                def out_fn(j, po, rd, h=h):
                    o = ost[k[0] % 2]; k[0] += 1
                    c.op('dve', lambda e: e.tensor_tensor(out=o[:], in0=po[:], in1=rd[:], op=ALU.mult), reads=[po, rd], writes=[o])
                    c.dma('sp', oxT[h * 128:(h + 1) * 128, j * 512:(j + 1) * 512], o[:], reads=[o])
                self.attention(es, qT, lambda i: (kT[:, i * 128:(i + 1) * 128], [kT], 128), lambda i: (vt[:, i, :], [vt]),
                               lambda j: [0, 1], 128 ** -0.5, lambda i, j: [], out_fn)
        c.barrier()
        with ExitStack() as es:
            self.linear(I["w_o_x"][l], 4, S, 2048, [[0, 1, 2, 3], [4, 5, 6, 7], [8, 9, 10, 11], [12, 13, 14, 15]], self.epi_residual(es, hT), src=oxT)

    def ffn(self, l):
        c = self.c; I = self.inp
        hT = self.scr["hT"]
        aT = self.dscr("aT", [DFF, S], BF16)
        NT = 2048
        with ExitStack() as es:
            cw = self.sb(es, [128, 3, 88]); cb = self.sb(es, [128, 88])
            for i3 in range(3):
                c.dma('sp', cw[:, i3, :], I["conv_w"][l][i3].rearrange("(k p) -> p k", p=128), writes=[cw])
            c.dma('sp', cb[:], I["conv_b"][l].rearrange("(k p) -> p k", p=128), writes=[cb])
            halo = self.sb(es, [128, 88, 2])
            c.op('dve', lambda e: e.memset(halo[:], 0.0), writes=[halo])
            U = [self.sb(es, [128, 514]) for _ in range(2)]
            cv = [self.sb(es, [128, 512]) for _ in range(3)]
            gst = self.sb(es, [128, 2, NT])
            ast = [self.sb(es, [128, 512], BF16) for _ in range(2)]
            k = [0]

            pend = [None]; ka = [0]

            def flush():
                if pend[0] is not None:
                    f = pend[0]; pend[0] = None
                    f()

            def epi(cc, t0, TB, p):
                u = U[k[0] % 2]; v = cv[k[0] % 3]; k[0] += 1
                tl = t0 % NT
                c.op('act', lambda e: e.copy(out=u[:, 2:514], in_=p[:, :]), reads=[p], writes=[u.sub('m')])
                flush()
                c.op('pool', lambda e: e.tensor_copy(out=u[:, 0:2], in_=halo[:, cc, :]), reads=[halo.sub(cc)], writes=[u.sub('h')])
                c.op('pool', lambda e: e.tensor_copy(out=halo[:, cc, :], in_=u[:, 512:514]), reads=[u.sub('m')], writes=[halo.sub(cc)])
                c.op('dve', lambda e: e.tensor_scalar(out=v[:], in0=u[:, 2:514], scalar1=cw[:, 2, cc:cc + 1], scalar2=cb[:, cc:cc + 1], op0=ALU.mult, op1=ALU.add),
                     reads=[u.sub('m'), cw, cb], writes=[v])
                c.op('dve', lambda e: e.scalar_tensor_tensor(out=v[:], in0=u[:, 1:513], scalar=cw[:, 1, cc:cc + 1], in1=v[:], op0=ALU.mult, op1=ALU.add),
                     reads=[u, cw, v], writes=[v])
                c.op('dve', lambda e: e.scalar_tensor_tensor(out=v[:], in0=u[:, 0:512], scalar=cw[:, 0, cc:cc + 1], in1=v[:], op0=ALU.mult, op1=ALU.add),
                     reads=[u, cw, v], writes=[v])

                def tail():
                    if cc < 44:
                        c.op('act', lambda e: e.activation(out=gst[:, cc % 2, tl:tl + 512], in_=v[:], func=AF.Silu), reads=[v], writes=[gst.sub((cc % 2, tl))])
                    else:
                        a = ast[ka[0] % 2]; ka[0] += 1
                        c.op('pool', lambda e: e.tensor_tensor(out=a[:], in0=gst[:, cc % 2, tl:tl + 512], in1=v[:], op=ALU.mult), reads=[gst.sub((cc % 2, tl)), v], writes=[a])
                        c.dma('sp', aT[(cc - 44) * 128:(cc - 43) * 128, t0:t0 + 512], a[:], reads=[a])
                pend[0] = tail
            epi.flush = flush
            groups = [[2 * pp, 2 * pp + 1, 44 + 2 * pp, 45 + 2 * pp] for pp in range(22)]
            self.linear(I["w_up"][l], 16, S, NT, groups, epi, norm=(hT, I["norm_ffn"][l]), CGMAX=512)
        with ExitStack() as es:
            self.linear(I["w_down"][l], 44, S, 1024, [[2 * i, 2 * i + 1] for i in range(8)], self.epi_residual(es, hT), src=aT, CGMAX=256)

    def final(self):
        c = self.c; I = self.inp
        hT = self.scr["hT"]; out = self.out
        with ExitStack() as es:
            XT = self.sb(es, [128, 16, 256], F32)
            st = {}
            ot = [self.sb(es, [128, D]) for _ in range(2)]
            for blk in range(S // 256):
                self.norm_input(es, hT, I["norm_final"], blk * 256, 256, XT, st)
                for t2 in range(2):
                    o = ot[(blk * 2 + t2) % 2]
                    for q4 in range(4):
                        p = self.ps()
                        for j in range(4):
                            kc = q4 * 4 + j
                            c.op('pe', lambda e, p=p, j=j, kc=kc: e.transpose(p[:, j * 128:(j + 1) * 128], XT[:, kc, t2 * 128:(t2 + 1) * 128], self.idf[:]),
                                 reads=[XT, self.idf], writes=[p], inc=(j == 3))
                        c.op('act', lambda e, p=p, q4=q4: e.copy(out=o[:, q4 * 512:(q4 + 1) * 512], in_=p[:]), reads=[p], writes=[o])
                    tok = blk * 256 + t2 * 128
                    c.dma('sp', out[tok:tok + 128, :], o[:], reads=[o])
        c.barrier()

    def build(self):
        nc = self.nc
        I = self.inp
        self.din("x", [S, D]); self.din("mem", [NMEM, D]); self.din("positions", [S], I32)
        for n in ("norm_mix", "norm_cross", "norm_ffn"):
            self.din(n, [DEPTH, D])
        self.din("norm_mem", [D]); self.din("norm_final", [D])
        self.din("w_q_x", [DEPTH, D, 512]); self.din("w_kv_x", [DEPTH, D, 1024]); self.din("w_o_x", [DEPTH, 512, D])
        self.din("w_up", [DEPTH, D, 2 * DFF]); self.din("conv_w", [DEPTH, 3, 2 * DFF]); self.din("conv_b", [DEPTH, 2 * DFF])
        self.din("w_down", [DEPTH, DFF, D])
        self.din("c_ident", [128, 128])
        self.din("w_in_c", [2, D, 12288]); self.din("ret_gn", [2, 8, 512]); self.din("w_out_c", [2, 4096, D])
        self.din("w_in_ab", [2, D, 5656]); self.din("w_out_ab", [2, D, D])
        for nm in ("cmp_pe_k", "cmp_pe_v"):
            self.din(nm, [2, 32, 128])
        for nm in ("cmp_w1_k", "cmp_w1_v"):
            self.din(nm, [2, 4096, 128])
        for nm in ("cmp_w2_k", "cmp_w2_v"):
            self.din(nm, [2, 128, 128])
        self.din("c_perm", [32, 32]); self.din("c_inv32", [32, 1]); self.din("c_mobavalid", [128, 512]); self.din("c_mobaown", [128, 512])
        self.din("c_E16", [16, S]); self.din("c_causal", [4, 128, 512]); self.din("c_cmpbias", [2, 128, S]); self.din("c_cover", [2, 128, 64])
        self.din("c_gsel", [24, 3072]); self.din("c_selkeep", [128, 32, 64]); self.din("c_seladd", [128, 32, 64]); self.din("c_E64", [64, S])
        self.din("c_winbias", [8, 128, 512])
        self.din("c_decT", [8, 128, 128]); self.din("c_qdec", [8, 128, 128]); self.din("c_kdec", [128, 8]); self.din("c_invR", [128, 1])
        self.out = nc.dram_tensor("out", [S, D], F32, kind="ExternalOutput").ap()
        self.dscr("hT", [D, S], F32); self.dscr("memT", [D, NMEM], F32)
        self.setup()
        self.to_feature_major(I["x"], self.scr["hT"], S)
        self.to_feature_major(I["mem"], self.scr["memT"], NMEM)
        self.dscr("cosR", [128, S], F32); self.dscr("sinR", [128, S], F32)
        self.rope_tables(I["c_invR"], 128, self.scr["cosR"], self.scr["sinR"])
        self.dscr("cos32", [32, S], F32); self.dscr("sin32", [32, S], F32)
        self.rope_tables(I["c_inv32"], 32, self.scr["cos32"], self.scr["sin32"])
        for l in range(self.nlayers):
            if "mix" in self.parts:
                if l % 2 == 1:
                    self.mixer_c(l)
                else:
                    self.mixer_ab(l)
            if "x" in self.parts:
                self.cross_attn(l)
            if "ffn" in self.parts:
                self.ffn(l)
        self.final()
        self.c.barrier()
        self.es.close()
        return nc


def host_consts():
    cs = {"c_ident": np.eye(128, dtype=np.float32)}
    n = np.arange(128, dtype=np.float64)
    decT = np.zeros((8, 128, 128)); qdec = np.zeros((8, 128, 128)); kdec = np.zeros((128, 8))
    for h in range(8):
        lg = np.log(1.0 - 2.0 ** (-5.0 - h))
        diff = n[None, :] - n[:, None]
        decT[h] = np.where(diff >= 0, np.exp(np.maximum(diff, 0) * lg), 0.0) / 16.0
        qdec[h] = np.exp((n + 1.0) * lg)[None, :].repeat(128, 0)
        kdec[:, h] = np.exp((127.0 - n) * lg) / 16.0
    cs["c_decT"] = decT.astype(np.float32); cs["c_qdec"] = qdec.astype(np.float32); cs["c_kdec"] = kdec.astype(np.float32)
    pm = np.zeros((32, 32), np.float32)
    for d in range(32):
        pm[(d + 16) % 32, d] = -1.0 if d < 16 else 1.0
    cs["c_perm"] = pm
    cs["c_inv32"] = (500000.0 ** (-(np.arange(32) % 16) / 16.0)).astype(np.float32).reshape(32, 1)
    ql = np.arange(128)[:, None, None]; qt = np.arange(32)[None, :, None]
    n16 = np.arange(16)[None, None, :]
    cs["c_mobavalid"] = np.where(n16 < (qt // 2) + 0 * ql, 0.0, -1e30).astype(np.float32).reshape(128, 512)
    cs["c_mobaown"] = np.where(n16 == (qt // 2) + 0 * ql, 1.0, 0.0).astype(np.float32).reshape(128, 512)
    sidx = np.arange(S)
    cs["c_E16"] = (sidx[None, :] // 256 == np.arange(16)[:, None]).astype(np.float32)
    cs["c_E64"] = (sidx[None, :] // 64 == np.arange(64)[:, None]).astype(np.float32)
    kk_ = np.arange(128)[None, :, None]; qq = np.arange(512)[None, None, :]
    m4 = np.arange(4)[:, None, None]
    cs["c_causal"] = np.where(128 * m4 + kk_ <= qq, 0.0, NEGB).astype(np.float32)
    m8 = (np.arange(8) - 4)[:, None, None]
    dist = qq - (128 * m8 + kk_)
    cs["c_winbias"] = np.where((dist >= 0) & (dist < 512), 0.0, NEGB).astype(np.float32)
    nn = (np.arange(2)[:, None, None] * 128 + np.arange(128)[None, :, None])
    cs["c_cmpbias"] = np.where((16 * nn + 31 <= sidx[None, None, :]) & (nn < 255), 0.0, NEGB).astype(np.float32)
    jj = np.arange(64)[None, None, :]
    cs["c_cover"] = ((nn >= 4 * jj - 1) & (nn <= 4 * jj + 3) & (nn < 255)).astype(np.float32)
    gs = np.zeros((24, 24, 128), np.float32)
    for cidx in range(24):
        gs[cidx, cidx, :] = 1.0
    cs["c_gsel"] = gs.reshape(24, 3072)
    q_ = qt * 128 + ql
    blk = q_ // 64
    j64 = np.arange(64)[None, None, :]
    is0 = (j64 == 0); isb = (j64 == blk); isb1 = (j64 == blk - 1)
    forced = is0 | isb | isb1
    cs["c_selkeep"] = ((~forced) & (j64 <= blk)).astype(np.float32)
    add = np.where(j64 > blk, -1e30, 0.0)
    add = np.where(is0, 1e9, add); add = np.where(isb1, 2e9, add); add = np.where(isb, 3e9, add)
    cs["c_seladd"] = add.astype(np.float32)
    cs["c_invR"] = (10000.0 ** (-np.arange(128) / 128.0)).astype(np.float32).reshape(128, 1)
    return cs


def kernel(**inputs):
    prog = Prog()
    nc = prog.build()
    consts = host_consts()
    in_maps = []
    for core in range(8):
        b = core // 2
        m = {}
        for name in prog.inp:
            if name in consts:
                m[name] = consts[name]
            elif name in ("x", "mem", "positions"):
                m[name] = np.ascontiguousarray(inputs[name][b])
            else:
                m[name] = np.ascontiguousarray(inputs[name])
        in_maps.append(m)
    res = run_bass_kernel_spmd(nc, in_maps, core_ids=list(range(8)))
    out = np.empty((4, S, D), np.float32)
    for b in range(4):
        out[b, :S // 2] = res.results[2 * b]["out"][:S // 2]
        out[b, S // 2:] = res.results[2 * b + 1]["out"][S // 2:]
    return out
```

```python
import numpy as np
from contextlib import ExitStack
import concourse.bass as bass
import concourse.mybir as mybir
from concourse.bass_utils import run_bass_kernel_spmd

F32 = mybir.dt.float32; BF16 = mybir.dt.bfloat16; I32 = mybir.dt.int32
AF = mybir.ActivationFunctionType; ALU = mybir.AluOpType; AX = mybir.AxisListType

S = 4096; D = 2048; NMEM = 256; DEPTH = 4; DFF = 5632
EPS = 1e-6
NEGB = -30000.0


class Buf:
    def __init__(self, t, parent=None):
        self.t = t; self.w = {}; self.r = {}; self.parent = parent; self.kids = {}

    def sub(self, key):
        if key not in self.kids:
            self.kids[key] = Buf(self.t, self)
        return self.kids[key]

    def __getitem__(self, k):
        return self.t[k]

    def fam(self):
        out = [self]
        if self.parent is not None:
            out.append(self.parent)
        out.extend(self.kids.values())
        return out


class Ctx:
    NDS = {'sp': 24, 'pool': 12, 'act': 6}

    def __init__(self, nc):
        self.nc = nc; self.es = ExitStack()
        self.eng = {'pe': nc.tensor, 'act': nc.scalar, 'dve': nc.vector, 'pool': nc.gpsimd, 'sp': nc.sync}
        self.psem = {}; self.cnt = {}
        for e in ('pe', 'act', 'dve', 'pool'):
            self.psem[e] = self.es.enter_context(nc.semaphore("ps_" + e)); self.cnt[e] = 0
        self.dsem = {}; self.dtot = {}; self.dnext = {}
        for q, n in self.NDS.items():
            self.dsem[q] = [self.es.enter_context(nc.semaphore(f"ds_{q}{i}")) for i in range(n)]
            self.dtot[q] = [0] * n; self.dnext[q] = 0
        self.waited = {e: {} for e in self.eng}
        self.nuid = 0

    def uid(self, p="t"):
        self.nuid += 1
        return f"{p}{self.nuid}"

    def wait(self, en, sem, val):
        if val <= 0:
            return
        key = sem.name
        if self.waited[en].get(key, 0) >= val:
            return
        self.waited[en][key] = val
        self.eng[en].wait_ge(sem, val)

    def _deps(self, en, reads, writes):
        deps = []
        for b in reads:
            for f in b.fam():
                deps.extend(f.w.values())
        for b in writes:
            for f in b.fam():
                deps.extend(f.w.values()); deps.extend(f.r.values())
        for (s, v) in deps:
            if en == 'pe' and s is self.psem['pe']:
                continue
            self.wait(en, s, v)

    def _record(self, ev, reads, writes):
        k = ev[0].name
        for b in reads:
            if b.r.get(k, (None, 0))[1] < ev[1]:
                b.r[k] = ev
        for b in writes:
            b.w = {k: ev}; b.r = {}

    def op(self, en, emit, reads=(), writes=(), inc=True):
        self._deps(en, reads, writes)
        ins = emit(self.eng[en])
        if inc:
            self.cnt[en] += 1
            ins.then_inc(self.psem[en], 1)
            ev = (self.psem[en], self.cnt[en])
        else:
            ev = (self.psem[en], self.cnt[en] + 1)
        self._record(ev, reads, writes)
        return ins

    def dma(self, q, out, in_, reads=(), writes=()):
        self._deps(q, reads, writes)
        i = self.dnext[q]; self.dnext[q] = (i + 1) % len(self.dsem[q])
        sem = self.dsem[q][i]
        self.wait(q, sem, self.dtot[q][i])
        self.eng[q].dma_start(out=out, in_=in_).then_inc(sem, 16)
        self.dtot[q][i] += 16
        self._record((sem, self.dtot[q][i]), reads, writes)

    def barrier(self):
        for en in self.eng:
            for e2, s in self.psem.items():
                if e2 != en:
                    self.wait(en, s, self.cnt[e2])
            for q in self.dsem:
                for s, t in zip(self.dsem[q], self.dtot[q]):
                    self.wait(en, s, t)


class Prog:
    def __init__(self, nlayers=DEPTH, parts=("mix", "x", "ffn")):
        self.nlayers = nlayers; self.parts = parts
        nc = bass.Bass("TRN2", target_bir_lowering=False)
        self.nc = nc
        self.c = Ctx(nc)
        self.es = self.c.es
        self.es.enter_context(nc.allow_non_contiguous_dma(reason="small strided loads"))
        self.inp = {}
        self.scr = {}

    def din(self, name, shape, dt=F32):
        self.inp[name] = self.nc.dram_tensor(name, list(shape), dt, kind="ExternalInput").ap()
        return self.inp[name]

    def dscr(self, name, shape, dt):
        if name not in self.scr:
            self.scr[name] = self.nc.dram_tensor(name, list(shape), dt, kind=("ExternalOutput" if name in getattr(self, "dbg", ()) else "Internal")).ap()
        return self.scr[name]

    def sb(self, es, shape, dt=F32):
        return Buf(es.enter_context(self.nc.sbuf_tensor(self.c.uid("sb"), list(shape), dt)))

    def setup(self):
        nc, c = self.nc, self.c
        self.PS = [Buf(self.es.enter_context(nc.psum_tensor(f"psb{i}", [128, 512], F32))) for i in range(8)]
        self.psi = 0
        self.idf = self.sb(self.es, [128, 128]); self.idb = self.sb(self.es, [128, 128], BF16)
        self.onef = self.sb(self.es, [128, 128]); self.oneb = self.sb(self.es, [128, 128], BF16)
        self._pt = [self.sb(self.es, [128, 512], BF16) for _ in range(3)]
        self._pti = 0
        self._rden = [self.sb(self.es, [128, 512]) for _ in range(2)]
        self._rdi = 0
        c.dma('sp', self.idf[:], self.inp["c_ident"], writes=[self.idf])
        c.op('dve', lambda e: e.tensor_copy(out=self.idb[:], in_=self.idf[:]), reads=[self.idf], writes=[self.idb])
        c.op('dve', lambda e: e.memset(self.onef[:], 1.0), writes=[self.onef])
        c.op('dve', lambda e: e.memset(self.oneb[:], 1.0), writes=[self.oneb])

    def ps(self):
        p = self.PS[self.psi]; self.psi = (self.psi + 1) % 4
        return p

    def to_feature_major(self, src, dst, ntok):
        c = self.c
        with ExitStack() as es:
            xin = [self.sb(es, [128, D]) for _ in range(2)]
            xo = [self.sb(es, [128, 16, 128]) for _ in range(2)]
            for tt in range(ntok // 128):
                a = xin[tt % 2]; o = xo[tt % 2]
                c.dma('sp', a[:], src[tt * 128:(tt + 1) * 128, :], writes=[a])
                for q4 in range(4):
                    p = self.ps()
                    for j in range(4):
                        kc = q4 * 4 + j
                        c.op('pe', lambda e, p=p, j=j, kc=kc: e.transpose(p[:, j * 128:(j + 1) * 128], a[:, kc * 128:(kc + 1) * 128], self.idf[:]),
                             reads=[a, self.idf], writes=[p], inc=(j == 3))
                    c.op('act' if q4 % 2 else 'dve',
                         (lambda e, p=p, q4=q4: e.copy(out=o[:, q4 * 4:(q4 + 1) * 4, :], in_=p[:].rearrange("p (j t) -> p j t", j=4))) if q4 % 2 else
                         (lambda e, p=p, q4=q4: e.tensor_copy(out=o[:, q4 * 4:(q4 + 1) * 4, :], in_=p[:].rearrange("p (j t) -> p j t", j=4))),
                         reads=[p], writes=[o])
                c.dma('sp', dst.rearrange("(kc p) s -> p kc s", p=128)[:, :, tt * 128:(tt + 1) * 128], o[:], reads=[o])
        c.barrier()

    def norm_input(self, es, hT, gam, t0, NT, XT, st):
        c = self.c
        if 'gt' not in st:
            st['gt'] = self.sb(es, [128, 16]); st['hb'] = [self.sb(es, [128, 16, 256]) for _ in range(2)]
            st['sq'] = self.sb(es, [128, 16, 256]); st['rs'] = self.sb(es, [128, 256])
            c.dma('sp', st['gt'][:], gam.rearrange("(kc p) -> p kc", p=128), writes=[st['gt']])
        gt = st['gt']; sq = st['sq']; rs = st['rs']
        BW = min(256, NT)
        for blk in range(NT // BW):
            hb = st['hb'][blk % 2]
            c.dma('sp', hb[:, :, 0:BW], hT.rearrange("(kc p) s -> p kc s", p=128)[:, :, t0 + blk * BW:t0 + (blk + 1) * BW], writes=[hb])
            c.op('act', lambda e: e.activation(out=sq[:, :, 0:BW], in_=hb[:, :, 0:BW], func=AF.Square), reads=[hb], writes=[sq])
            p = self.ps()
            for kc in range(16):
                c.op('pe', lambda e, kc=kc: e.matmul(p[:, 0:BW], self.onef[:], sq[:, kc, 0:BW], start=(kc == 0), stop=(kc == 15)),
                     reads=[sq, self.onef], writes=[p], inc=(kc == 15))
            c.op('dve', lambda e: e.tensor_scalar(out=rs[:, 0:BW], in0=p[:, 0:BW], scalar1=1.0 / D, scalar2=EPS, op0=ALU.mult, op1=ALU.add), reads=[p], writes=[rs])
            c.op('act', lambda e: e.activation(out=rs[:, 0:BW], in_=rs[:, 0:BW], func=AF.Sqrt), reads=[rs], writes=[rs])
            c.op('dve', lambda e: e.reciprocal(out=rs[:, 0:BW], in_=rs[:, 0:BW]), reads=[rs], writes=[rs])
            for kc in range(16):
                en = 'dve'
                c.op(en, lambda e, kc=kc: e.scalar_tensor_tensor(out=XT[:, kc, blk * BW:(blk + 1) * BW], in0=hb[:, kc, 0:BW], scalar=gt[:, kc:kc + 1],
                                                                  in1=rs[:, 0:BW], op0=ALU.mult, op1=ALU.mult), reads=[hb, rs, gt], writes=[XT.sub((blk * BW) // 512)])

    def linear(self, W, KC, ntok, NT, groups, epi, norm=None, src=None, CGMAX=512):
        c = self.c
        Wv = W.rearrange("(kc p) c -> p kc c", p=128)
        with ExitStack() as es:
            XT = self.sb(es, [128, KC, NT], BF16)
            wts = [self.sb(es, [128, KC, CGMAX], BF16) for _ in range(2)]
            st = {}
            WC = W.shape[1]
            items = [(g, gi) for g in range(ntok // NT) for gi in range(len(groups))]

            def load_w(k):
                g, gi = items[k]
                grp = groups[gi]; wt = wts[k % 2]
                runs = []
                for i, cc in enumerate(grp):
                    if runs and runs[-1][1] + runs[-1][2] == cc:
                        runs[-1][2] += 1
                    else:
                        runs.append([i, cc, 1])
                for (i0, cc0, n) in runs:
                    wd = min((cc0 + n) * 128, WC) - cc0 * 128
                    c.dma('pool', wt[:, :, i0 * 128:i0 * 128 + wd], Wv[:, :, cc0 * 128:cc0 * 128 + wd], writes=[wt])
            load_w(0)
            for k, (g, gi) in enumerate(items):
                grp = groups[gi]; wt = wts[k % 2]
                if gi == 0:
                    if norm is not None:
                        self.norm_input(es, norm[0], norm[1], g * NT, NT, XT, st)
                    else:
                        for tb in range(NT // 512):
                            c.dma('sp', XT[:, :, tb * 512:(tb + 1) * 512], src.rearrange("(kc p) s -> p kc s", p=128)[:, :, g * NT + tb * 512:g * NT + (tb + 1) * 512], writes=[XT.sub(tb)])
                if k + 1 < len(items):
                    load_w(k + 1)
                for i, cc in enumerate(grp):
                    TB = min(512, NT)
                    for tb in range(NT // TB):
                        if hasattr(epi, 'pre'):
                            epi.pre(cc, g * NT + tb * TB, TB)
                        p = self.ps()
                        M = min(128, WC - cc * 128)
                        for kc in range(KC):
                            c.op('pe', lambda e: e.matmul(p[0:M, 0:TB], wt[:, kc, i * 128:i * 128 + M], XT[:, kc, tb * TB:(tb + 1) * TB],
                                                          start=(kc == 0), stop=(kc == KC - 1)),
                                 reads=[wt, XT.sub(tb)], writes=[p], inc=(kc == KC - 1))
                        epi(cc, g * NT + tb * TB, TB, p)
            if hasattr(epi, 'flush'):
                epi.flush()
        c.barrier()

    def epi_store(self, es, dst, dt, scale=None):
        c = self.c
        stg = [self.sb(es, [128, 512], dt) for _ in range(3)]
        k = [0]

        def epi(cc, t0, TB, p, row0=None):
            s = stg[k[0] % 3]; k[0] += 1
            if scale is None:
                c.op('act', lambda e: e.copy(out=s[:, 0:TB], in_=p[:, 0:TB]), reads=[p], writes=[s])
            else:
                c.op('act', lambda e: e.mul(out=s[:, 0:TB], in_=p[:, 0:TB], mul=scale), reads=[p], writes=[s])
            r0 = cc * 128 if row0 is None else row0
            c.dma('sp', dst[r0:r0 + 128, t0:t0 + TB], s[:, 0:TB], reads=[s])
        return epi

    def epi_residual(self, es, hT):
        c = self.c
        stg = [self.sb(es, [128, 512]) for _ in range(3)]
        k = [0]

        cur = {}

        def pre(cc, t0, TB):
            s = stg[k[0] % 3]; k[0] += 1
            cur[(cc, t0)] = s
            c.dma('sp', s[:, 0:TB], hT[cc * 128:(cc + 1) * 128, t0:t0 + TB], writes=[s])

        def epi(cc, t0, TB, p):
            s = cur.pop((cc, t0))
            c.op('dve', lambda e: e.tensor_tensor(out=s[:, 0:TB], in0=s[:, 0:TB], in1=p[:, 0:TB], op=ALU.add), reads=[p, s], writes=[s])
            c.dma('sp', hT[cc * 128:(cc + 1) * 128, t0:t0 + TB], s[:, 0:TB], reads=[s])
        epi.pre = pre
        return epi

    def attention(self, es, qT, kT_tiles, v_tiles, units, scale, bias_fn, out_fn, extra_fn=None, clamp=False):
        c = self.c
        nq = qT.t.shape[-1] // 512
        for j in range(nq):
            ulist = units(j)
            po = self.PS[4 + (self._rdi % 2)]; pd = self.PS[6 + (self._rdi % 2)]
            nu = len(ulist)

            def emit_scores(ui):
                i = ulist[ui]
                ka, kb, nk = kT_tiles(i)
                bl = bias_fn(i, j)
                p = self.ps()
                c.op('pe', lambda e: e.matmul(p[0:nk, :], ka, qT[:, j * 512:(j + 1) * 512], start=True, stop=(len(bl) == 0)),
                     reads=[qT] + kb, writes=[p], inc=(len(bl) == 0))
                for bi, (bl_l, bl_r, bb) in enumerate(bl):
                    c.op('pe', lambda e: e.matmul(p[0:nk, :], bl_l, bl_r, start=False, stop=(bi == len(bl) - 1)),
                         reads=bb, writes=[p], inc=(bi == len(bl) - 1))
                pt = self._pt[self._pti % 3]; self._pti += 1
                c.op('act', lambda e: e.activation(out=pt[0:nk, :], in_=p[0:nk, :], func=AF.Exp, scale=scale), reads=[p], writes=[pt])
                return pt, nk

            def emit_pv(ui, pt, nk):
                i = ulist[ui]
                va, vb = v_tiles(i)
                last = (ui == nu - 1)
                c.op('pe', lambda e: e.matmul(po[:, :], va, pt[0:nk, :], start=(ui == 0), stop=last), reads=[pt] + vb, writes=[po], inc=last)
                c.op('pe', lambda e: e.matmul(pd[:, :], self.oneb[0:nk, :], pt[0:nk, :], start=(ui == 0), stop=last), reads=[pt, self.oneb], writes=[pd], inc=last)
                if extra_fn is not None:
                    extra_fn(i, j, ui, nu, pt, nk)
            prev = None
            for ui in range(nu):
                cur = emit_scores(ui)
                if prev is not None:
                    emit_pv(ui - 1, *prev)
                prev = cur
            emit_pv(nu - 1, *prev)
            rd = self._rden[self._rdi % 2]; self._rdi += 1
            if clamp:
                c.op('dve', lambda e: e.tensor_scalar_max(out=rd[:], in0=pd[:], scalar1=1e-30), reads=[pd], writes=[rd])
                c.op('dve', lambda e: e.reciprocal(out=rd[:], in_=rd[:]), reads=[rd], writes=[rd])
            else:
                c.op('dve', lambda e: e.reciprocal(out=rd[:], in_=pd[:]), reads=[pd], writes=[rd])
            out_fn(j, po, rd)

    def load_tokmajor(self, vsrc_rows, ntok, vt, vf, co=0):
        c = self.c
        if True:
            c.dma('sp', vf[:, 0:ntok], vsrc_rows, writes=[vf])
            for t4 in range(ntok // 512):
                p = self.ps()
                for j in range(4):
                    tt = t4 * 4 + j
                    c.op('pe', lambda e, j=j, tt=tt: e.transpose(p[:, j * 128:(j + 1) * 128], vf[:, tt * 128:(tt + 1) * 128], self.idf[:]),
                         reads=[vf, self.idf], writes=[p], inc=(j == 3))
                c.op('dve', lambda e: e.tensor_copy(out=vt[:, t4 * 4:(t4 + 1) * 4, co:co + 128], in_=p[:].rearrange("p (j d) -> p j d", j=4)), reads=[p], writes=[vt])
            if ntok % 512:
                base = (ntok // 512) * 4
                n = (ntok % 512) // 128
                p = self.ps()
                for j in range(n):
                    tt = base + j
                    c.op('pe', lambda e, j=j, tt=tt: e.transpose(p[:, j * 128:(j + 1) * 128], vf[:, tt * 128:(tt + 1) * 128], self.idf[:]),
                         reads=[vf, self.idf], writes=[p], inc=(j == n - 1))
                c.op('dve', lambda e: e.tensor_copy(out=vt[:, base:base + n, co:co + 128], in_=p[:, 0:n * 128].rearrange("p (j d) -> p j d", j=n)), reads=[p], writes=[vt])

    def rope_tables(self, inv_ap, nP, cosD, sinD):
        c = self.c; I = self.inp
        with ExitStack() as es:
            inv = self.sb(es, [128, 1]); pi_ = self.sb(es, [128, 512], I32); pf = self.sb(es, [128, 512])
            ang = self.sb(es, [128, 512]); ki = self.sb(es, [128, 512], I32); kf = self.sb(es, [128, 512])
            r = self.sb(es, [128, 512]); m = self.sb(es, [128, 512]); o = [self.sb(es, [128, 512]) for _ in range(2)]
            c.dma('sp', inv[0:nP, :], inv_ap, writes=[inv])
            k = 0
            for tb in range(S // 512):
                c.dma('sp', pi_[0:nP, :], I["positions"][tb * 512:(tb + 1) * 512].partition_broadcast(nP), writes=[pi_])
                c.op('dve', lambda e: e.tensor_copy(out=pf[0:nP, :], in_=pi_[0:nP, :]), reads=[pi_], writes=[pf])
                for which, dst in ((0, sinD), (1, cosD)):
                    c.op('dve', lambda e: e.tensor_scalar(out=ang[0:nP, :], in0=pf[0:nP, :], scalar1=inv[0:nP, 0:1], scalar2=float(which * np.pi / 2), op0=ALU.mult, op1=ALU.add),
                         reads=[pf, inv], writes=[ang])
                    c.op('dve', lambda e: e.tensor_scalar(out=ki[0:nP, :], in0=ang[0:nP, :], scalar1=float(1 / (2 * np.pi)), scalar2=None, op0=ALU.mult), reads=[ang], writes=[ki])
                    c.op('dve', lambda e: e.tensor_copy(out=kf[0:nP, :], in_=ki[0:nP, :]), reads=[ki], writes=[kf])
                    c.op('dve', lambda e: e.scalar_tensor_tensor(out=r[0:nP, :], in0=kf[0:nP, :], scalar=float(-2 * np.pi), in1=ang[0:nP, :], op0=ALU.mult, op1=ALU.add),
                         reads=[kf, ang], writes=[r])
                    c.op('dve', lambda e: e.tensor_scalar(out=m[0:nP, :], in0=r[0:nP, :], scalar1=float(np.pi), scalar2=float(2 * np.pi), op0=ALU.is_gt, op1=ALU.mult), reads=[r], writes=[m])
                    c.op('dve', lambda e: e.tensor_tensor(out=r[0:nP, :], in0=r[0:nP, :], in1=m[0:nP, :], op=ALU.subtract), reads=[r, m], writes=[r])
                    c.op('dve', lambda e: e.tensor_scalar(out=r[0:nP, :], in0=r[0:nP, :], scalar1=3.14159, scalar2=-3.14159, op0=ALU.min, op1=ALU.max), reads=[r], writes=[r])
                    ob = o[k % 2]; k += 1
                    c.op('act', lambda e: e.activation(out=ob[0:nP, :], in_=r[0:nP, :], func=AF.Sin), reads=[r], writes=[ob])
                    c.dma('sp', dst[:, tb * 512:(tb + 1) * 512], ob[0:nP, :], reads=[ob])
        c.barrier()

    def mixer_c(self, l):
        c = self.c; I = self.inp; o = l // 2
        hT = self.scr["hT"]
        rqT = self.dscr("rqT", [2048, S], BF16); rkT = self.dscr("rkT", [2048, S], BF16)
        rvT = self.dscr("rvT", [4096, S], F32); rgT = self.dscr("rgT", [4096, S], BF16); zT = self.dscr("zT", [4096, S], BF16)
        cosR = self.scr["cosR"]; sinR = self.scr["sinR"]
        NT = 2048
        with ExitStack() as es:
            x1 = self.sb(es, [128, NT]); cs = self.sb(es, [128, NT]); sn = self.sb(es, [128, NT])
            ta = [self.sb(es, [128, 512]) for _ in range(2)]; tb_ = [self.sb(es, [128, 512]) for _ in range(2)]
            o1 = [self.sb(es, [128, 512], BF16) for _ in range(2)]; o2 = [self.sb(es, [128, 512], BF16) for _ in range(2)]
            ev = self.epi_store(es, rvT, F32)
            gs = [self.sb(es, [128, 512], BF16) for _ in range(2)]
            k = [0]; curg = [-1]

            def epi(cc, t0, TB, p):
                g = t0 // NT; tl = t0 % NT
                if cc < 32:
                    if curg[0] != g:
                        curg[0] = g
                        c.dma('sp', cs[:], cosR[:, g * NT:(g + 1) * NT], writes=[cs])
                        c.dma('sp', sn[:], sinR[:, g * NT:(g + 1) * NT], writes=[sn])
                    if cc % 2 == 0:
                        c.op('act', lambda e: e.copy(out=x1[:, tl:tl + 512], in_=p[:]), reads=[p], writes=[x1.sub(tl)])
                    else:
                        i = k[0] % 2; k[0] += 1
                        a = ta[i]; b = tb_[i]
                        dst = rqT if cc < 16 else rkT
                        r0 = (cc % 16 - 1) * 128
                        c.op('pool', lambda e: e.tensor_tensor(out=a[:], in0=x1[:, tl:tl + 512], in1=cs[:, tl:tl + 512], op=ALU.mult), reads=[x1.sub(tl), cs], writes=[a])
                        c.op('dve', lambda e: e.tensor_tensor(out=b[:], in0=p[:], in1=sn[:, tl:tl + 512], op=ALU.mult), reads=[p, sn], writes=[b])
                        c.op('dve', lambda e: e.tensor_tensor(out=o1[i][:], in0=a[:], in1=b[:], op=ALU.subtract), reads=[a, b], writes=[o1[i]])
                        c.dma('sp', dst[r0:r0 + 128, t0:t0 + 512], o1[i][:], reads=[o1[i]])
                        c.op('pool', lambda e: e.tensor_tensor(out=a[:], in0=x1[:, tl:tl + 512], in1=sn[:, tl:tl + 512], op=ALU.mult), reads=[x1.sub(tl), sn], writes=[a])
                        c.op('dve', lambda e: e.tensor_tensor(out=b[:], in0=p[:], in1=cs[:, tl:tl + 512], op=ALU.mult), reads=[p, cs], writes=[b])
                        c.op('dve', lambda e: e.tensor_tensor(out=o2[i][:], in0=a[:], in1=b[:], op=ALU.add), reads=[a, b], writes=[o2[i]])
                        c.dma('sp', dst[r0 + 128:r0 + 256, t0:t0 + 512], o2[i][:], reads=[o2[i]])
                elif cc < 64:
                    ev(cc, t0, TB, p, row0=(cc - 32) * 128)
                else:
                    i = k[0] % 2; k[0] += 1
                    c.op('act', lambda e: e.activation(out=gs[i][:], in_=p[:], func=AF.Silu), reads=[p], writes=[gs[i]])
                    c.dma('sp', rgT[(cc - 64) * 128:(cc - 63) * 128, t0:t0 + 512], gs[i][:], reads=[gs[i]])
            self.linear(I["w_in_c"][o], 16, S, NT, [[4 * i + j for j in range(4)] for i in range(24)], epi, norm=(hT, I["norm_mix"][l]))
        with ExitStack() as es:
            qT = self.sb(es, [128, 2, S], BF16); kT = self.sb(es, [128, 2, S], BF16); vt = self.sb(es, [128, 32, 512], BF16)
            gT = self.sb(es, [128, 4, S], BF16); vfr = self.sb(es, [128, S])
            DT = self.sb(es, [128, 128]); QD = self.sb(es, [128, 128]); KD = self.sb(es, [128, 8]); gain = self.sb(es, [128, 512])
            stf = self.sb(es, [128, 2, 512]); stb = self.sb(es, [128, 2, 512], BF16)
            ATm = [self.sb(es, [128, 128], BF16) for _ in range(2)]; qd = [self.sb(es, [128, 2, 128], BF16) for _ in range(2)]
            kd = [self.sb(es, [128, 256], BF16) for _ in range(2)]
            sm = [self.sb(es, [128, 8]) for _ in range(2)]; junk = self.sb(es, [128, 512])
            yn = [self.sb(es, [128, 512]) for _ in range(2)]
            zst = [self.sb(es, [128, 4, 512], BF16) for _ in range(2)]
            c.dma('sp', KD[:], I["c_kdec"], writes=[KD])
            for h in range(8):
                gam = 1.0 - 2.0 ** (-5.0 - h)
                cdec = float(gam ** 128)
                c.dma('sp', qT[:], rqT[h * 256:(h + 1) * 256, :].rearrange("(dc p) s -> p dc s", p=128), writes=[qT])
                c.dma('sp', kT[:], rkT[h * 256:(h + 1) * 256, :].rearrange("(dc p) s -> p dc s", p=128), writes=[kT])
                c.dma('sp', gT[:], rgT[h * 512:(h + 1) * 512, :].rearrange("(vc p) s -> p vc s", p=128), writes=[gT])
                c.dma('sp', DT[:], I["c_decT"][h], writes=[DT]); c.dma('sp', QD[:], I["c_qdec"][h], writes=[QD])
                c.dma('sp', gain[:], I["ret_gn"][o, h].partition_broadcast(128), writes=[gain])
                for vc in range(4):
                    self.load_tokmajor(rvT[h * 512 + vc * 128:h * 512 + (vc + 1) * 128, :], S, vt, vfr, co=vc * 128)
                c.op('dve', lambda e: e.memset(stf[:], 0.0), writes=[stf])
                c.op('dve', lambda e: e.memset(stb[:], 0.0), writes=[stb])
                def stA(ch):
                    n0 = ch * 128; i = ch % 2
                    pA = self.ps()
                    for dc in range(2):
                        c.op('pe', lambda e: e.matmul(pA[:, 0:128], kT[:, dc, n0:n0 + 128], qT[:, dc, n0:n0 + 128], start=(dc == 0), stop=(dc == 1)),
                             reads=[kT, qT], writes=[pA], inc=(dc == 1))
                    c.op('dve', lambda e: e.tensor_tensor(out=ATm[i][:], in0=pA[:, 0:128], in1=DT[:], op=ALU.mult), reads=[pA, DT], writes=[ATm[i]])
                    for dc in range(2):
                        c.op('pool', lambda e: e.tensor_tensor(out=qd[i][:, dc, :], in0=qT[:, dc, n0:n0 + 128], in1=QD[:], op=ALU.mult), reads=[qT, QD], writes=[qd[i]])
                    pK = self.ps()
                    for dc in range(2):
                        c.op('pe', lambda e: e.matmul(pK[:, dc * 128:(dc + 1) * 128], kT[:, dc, n0:n0 + 128], self.idb[:], start=True, stop=True),
                             reads=[kT, self.idb], writes=[pK], inc=(dc == 1))
                    c.op('act', lambda e: e.mul(out=kd[i][:], in_=pK[:, 0:256], mul=KD[:, h:h + 1]), reads=[pK, KD], writes=[kd[i]])

                def stB(ch):
                    i = ch % 2
                    pY = self.PS[4 + ch % 2]
                    c.op('pe', lambda e: e.matmul(pY[:], ATm[i][:], vt[:, ch, :], start=True, stop=False), reads=[ATm[i], vt], writes=[pY], inc=False)
                    for dc in range(2):
                        c.op('pe', lambda e: e.matmul(pY[:], qd[i][:, dc, :], stb[:, dc, :], start=False, stop=(dc == 1)), reads=[qd[i], stb], writes=[pY], inc=(dc == 1))
                    for dc in range(2):
                        pS = self.ps()
                        c.op('pe', lambda e: e.matmul(pS[:], kd[i][:, dc * 128:(dc + 1) * 128], vt[:, ch, :], start=True, stop=True), reads=[kd[i], vt], writes=[pS])
                        c.op('dve', lambda e: e.scalar_tensor_tensor(out=stf[:, dc, :], in0=stf[:, dc, :], scalar=cdec, in1=pS[:], op0=ALU.mult, op1=ALU.add),
                             reads=[stf, pS], writes=[stf])
                    c.op('act', lambda e: e.copy(out=stb[:], in_=stf[:]), reads=[stf], writes=[stb])
                    return pY

                def stCn(ch, pY):
                    i = ch % 2
                    s_ = sm[i]
                    c.op('dve', lambda e: e.reduce_sum(out=s_[:, 0:1], in_=pY[:], axis=AX.X), reads=[pY], writes=[s_])
                    c.op('act', lambda e: e.activation(out=junk[:], in_=pY[:], func=AF.Square, accum_out=s_[:, 1:2]), reads=[pY], writes=[junk, s_])
                    c.op('dve', lambda e: e.tensor_scalar(out=s_[:, 2:3], in0=s_[:, 0:1], scalar1=1.0 / 512, scalar2=None, op0=ALU.mult), reads=[s_], writes=[s_])
                    c.op('dve', lambda e: e.tensor_tensor(out=s_[:, 3:4], in0=s_[:, 2:3], in1=s_[:, 2:3], op=ALU.mult), reads=[s_], writes=[s_])
                    c.op('dve', lambda e: e.scalar_tensor_tensor(out=s_[:, 4:5], in0=s_[:, 1:2], scalar=1.0 / 512, in1=s_[:, 3:4], op0=ALU.mult, op1=ALU.subtract), reads=[s_], writes=[s_])
                    c.op('dve', lambda e: e.tensor_scalar(out=s_[:, 4:5], in0=s_[:, 4:5], scalar1=EPS, scalar2=None, op0=ALU.add), reads=[s_], writes=[s_])
                    c.op('act', lambda e: e.activation(out=s_[:, 5:6], in_=s_[:, 4:5], func=AF.Sqrt), reads=[s_], writes=[s_])
                    c.op('dve', lambda e: e.reciprocal(out=s_[:, 6:7], in_=s_[:, 5:6]), reads=[s_], writes=[s_])
                    y = yn[i]
                    c.op('dve', lambda e: e.scalar_tensor_tensor(out=s_[:, 7:8], in0=s_[:, 2:3], scalar=-1.0, in1=s_[:, 6:7], op0=ALU.mult, op1=ALU.mult), reads=[s_], writes=[s_])
                    c.op('act', lambda e: e.activation(out=y[:], in_=pY[:], func=AF.Identity, bias=s_[:, 7:8], scale=s_[:, 6:7]), reads=[pY, s_], writes=[y])
                    c.op('pool', lambda e: e.tensor_tensor(out=y[:], in0=y[:], in1=gain[:], op=ALU.mult), reads=[y, gain], writes=[y])

                def stCt(ch):
                    n0 = ch * 128; y = yn[ch % 2]
                    pT = self.PS[6 + ch % 2]
                    for vc in range(4):
                        c.op('pe', lambda e: e.transpose(pT[:, vc * 128:(vc + 1) * 128], y[:, vc * 128:(vc + 1) * 128], self.idf[:]), reads=[y, self.idf], writes=[pT], inc=(vc == 3))
                    zs = zst[(ch // 4) % 2]
                    c.op('dve', lambda e: e.tensor_tensor(out=zs[:, :, (ch % 4) * 128:(ch % 4 + 1) * 128], in0=pT[:].rearrange("p (v n) -> p v n", v=4),
                                                          in1=gT[:, :, n0:n0 + 128], op=ALU.mult), reads=[pT, gT], writes=[zs])
                    if ch % 4 == 3:
                        t0 = (ch // 4) * 512
                        c.dma('sp', zT[h * 512:(h + 1) * 512, t0:t0 + 512].rearrange("(vc p) s -> p vc s", p=128), zs[:], reads=[zs])

                stA(0)
                for ch in range(32):
                    if ch + 1 < 32:
                        stA(ch + 1)
                    pY_ = stB(ch)
                    stCn(ch, pY_)
                    if ch >= 1:
                        stCt(ch - 1)
                stCt(31)
        c.barrier()
        with ExitStack() as es:
            self.linear(I["w_out_c"][o], 32, S, 1024, [[4 * i + j for j in range(4)] for i in range(4)], self.epi_residual(es, hT), src=zT)

    def mixer_ab(self, l):
        c = self.c; I = self.inp; e_ = l // 2
        hT = self.scr["hT"]
        SC = 128 ** -0.5
        mqT = self.dscr("mqT", [1024, S], BF16); mkT = self.dscr("mkT", [1024, S], BF16); mvT = self.dscr("mvT", [1024, S], F32)
        nqT = self.dscr("nqT", [1024, S], BF16); nqrT = self.dscr("nqrT", [1024, S], BF16)
        nkcT = self.dscr("nkcT", [256, S], BF16); nvcT = self.dscr("nvcT", [256, S], BF16)
        nksT = self.dscr("nksT", [256, S], BF16); nvsT = self.dscr("nvsT", [256, S], F32)
        nkwT = self.dscr("nkwT", [256, S], BF16); nvwT = self.dscr("nvwT", [256, S], F32)
        ngT = self.dscr("ngT", [24, S], F32); oT = self.dscr("oT", [2048, S], BF16); ocT = self.dscr("ocT", [1024, S], F32)
        cos32 = self.scr["cos32"]; sin32 = self.scr["sin32"]
        NT = 1024
        with ExitStack() as es:
            cs = self.sb(es, [32, NT]); sn = self.sb(es, [32, NT]); pm = self.sb(es, [32, 32])
            xs = [self.sb(es, [128, 512]) for _ in range(2)]
            ta = [self.sb(es, [32, 512]) for _ in range(2)]; tb_ = [self.sb(es, [32, 512]) for _ in range(2)]
            ob = [self.sb(es, [128, 512], BF16) for _ in range(2)]
            gsg = [self.sb(es, [24, 512]) for _ in range(2)]
            c.dma('sp', pm[:], I["c_perm"], writes=[pm])
            e_bf = {}
            for nm, dst in (("mv", mvT), ("nvs", nvsT), ("nvw", nvwT)):
                e_bf[nm] = self.epi_store(es, dst, F32)
            for nm, dst in (("nq", nqT), ("nkc", nkcT), ("nvc", nvcT)):
                e_bf[nm] = self.epi_store(es, dst, BF16)
            k = [0]; curg = [-1]

            def rope_store(cc, t0, p, dst, r0):
                g = t0 // NT; tl = t0 % NT
                if curg[0] != g:
                    curg[0] = g
                    c.dma('sp', cs[:], cos32[:, g * NT:(g + 1) * NT], writes=[cs])
                    c.dma('sp', sn[:], sin32[:, g * NT:(g + 1) * NT], writes=[sn])
                i = k[0] % 2; k[0] += 1
                x = xs[i]; a = ta[i]; b = tb_[i]; o = ob[i]
                c.op('act', lambda e: e.copy(out=x[:], in_=p[:]), reads=[p], writes=[x])
                pp = self.ps()
                c.op('pe', lambda e: e.matmul(pp[0:32, :], pm[:], x[0:32, :], start=True, stop=True), reads=[pm, x], writes=[pp])
                c.op('pool', lambda e: e.tensor_tensor(out=a[:], in0=x[0:32, :], in1=cs[:, tl:tl + 512], op=ALU.mult), reads=[x, cs], writes=[a])
                c.op('dve', lambda e: e.tensor_tensor(out=b[:], in0=pp[0:32, :], in1=sn[:, tl:tl + 512], op=ALU.mult), reads=[pp, sn], writes=[b])
                c.op('act', lambda e: e.copy(out=o[:], in_=x[:]), reads=[x], writes=[o])
                c.op('dve', lambda e: e.tensor_tensor(out=o[0:32, :], in0=a[:], in1=b[:], op=ALU.add), reads=[a, b], writes=[o])
                c.dma('sp', dst[r0:r0 + 128, t0:t0 + 512], o[:], reads=[o])

            def epi(cc, t0, TB, p):
                if cc < 8:
                    rope_store(cc, t0, p, mqT, cc * 128)
                elif cc < 16:
                    rope_store(cc, t0, p, mkT, (cc - 8) * 128)
                elif cc < 24:
                    e_bf["mv"](cc, t0, TB, p, row0=(cc - 16) * 128)
                elif cc < 32:
                    e_bf["nq"](cc, t0, TB, p, row0=(cc - 24) * 128)
                    rope_store(cc, t0, p, nqrT, (cc - 24) * 128)
                elif cc < 34:
                    e_bf["nkc"](cc, t0, TB, p, row0=(cc - 32) * 128)
                elif cc < 36:
                    e_bf["nvc"](cc, t0, TB, p, row0=(cc - 34) * 128)
                elif cc < 38:
                    rope_store(cc, t0, p, nksT, (cc - 36) * 128)
                elif cc < 40:
                    e_bf["nvs"](cc, t0, TB, p, row0=(cc - 38) * 128)
                elif cc < 42:
                    rope_store(cc, t0, p, nkwT, (cc - 40) * 128)
                elif cc < 44:
                    e_bf["nvw"](cc, t0, TB, p, row0=(cc - 42) * 128)
                else:
                    i = k[0] % 2; k[0] += 1
                    c.op('act', lambda e: e.activation(out=gsg[i][:], in_=p[0:24, :], func=AF.Sigmoid), reads=[p], writes=[gsg[i]])
                    c.dma('sp', ngT[:, t0:t0 + 512], gsg[i][:], reads=[gsg[i]])
            groups = [[4 * i + j for j in range(4)] for i in range(11)] + [[44]]
            self.linear(I["w_in_ab"][e_], 16, S, NT, groups, epi, norm=(hT, I["norm_mix"][l]))
        with ExitStack() as es:
            qT = self.sb(es, [128, S], BF16); kT = self.sb(es, [128, S], BF16); vt = self.sb(es, [128, 32, 128], BF16); vf = self.sb(es, [128, S])
            E16 = self.sb(es, [16, S], BF16); caus = self.sb(es, [128, 4, 512], BF16); selbT = self.sb(es, [16, S], BF16)
            valid = self.sb(es, [128, 512]); own = self.sb(es, [128, 512])
            km = self.sb(es, [128, 16]); kmb = self.sb(es, [128, 16], BF16)
            gm = self.sb(es, [128, 32, 16]); t8 = self.sb(es, [128, 32, 8]); thr = self.sb(es, [128, 32, 1]); selm = self.sb(es, [128, 32, 16])
            ost = [self.sb(es, [128, 512], BF16) for _ in range(2)]
            c.dma('pool', E16[:], I["c_E16"], writes=[E16])
            c.dma('pool', caus[:], I["c_causal"].rearrange("m k q -> k m q"), writes=[caus])
            c.dma('sp', valid[:], I["c_mobavalid"], writes=[valid]); c.dma('sp', own[:], I["c_mobaown"], writes=[own])
            kk = [0]
            for h in range(8):
                c.dma('sp', qT[:], mqT[h * 128:(h + 1) * 128, :], writes=[qT])
                c.dma('sp', kT[:], mkT[h * 128:(h + 1) * 128, :], writes=[kT])
                self.load_tokmajor(mvT[h * 128:(h + 1) * 128, :], S, vt, vf)
                c.op('dve', lambda e: e.tensor_reduce(out=km[:], in_=kT[:].rearrange("p (n k) -> p n k", k=256), axis=AX.X, op=ALU.add), reads=[kT], writes=[km])
                c.op('dve', lambda e: e.tensor_scalar(out=kmb[:], in0=km[:], scalar1=1.0 / 256, scalar2=None, op0=ALU.mult), reads=[km], writes=[kmb])
                pg = self.ps()
                for qt in range(32):
                    c.op('pe', lambda e: e.matmul(pg[:, qt * 16:(qt + 1) * 16], qT[:, qt * 128:(qt + 1) * 128], kmb[:], start=True, stop=True),
                         reads=[qT, kmb], writes=[pg], inc=(qt == 31))
                c.op('dve', lambda e: e.tensor_tensor(out=gm[:].rearrange("p a b -> p (a b)"), in0=pg[:], in1=valid[:], op=ALU.add), reads=[pg, valid], writes=[gm])
                for qt in range(32):
                    c.op('dve', lambda e: e.max(out=t8[:, qt, :], in_=gm[:, qt, :]), reads=[gm], writes=[t8])
                c.op('dve', lambda e: e.tensor_scalar_max(out=thr[:], in0=t8[:, :, 2:3], scalar1=-1e29), reads=[t8], writes=[thr])
                c.op('dve', lambda e: e.tensor_tensor(out=selm[:], in0=gm[:], in1=thr[:].to_broadcast([128, 32, 16]), op=ALU.is_ge), reads=[gm, thr], writes=[selm])
                c.op('dve', lambda e: e.tensor_tensor(out=selm[:].rearrange("p a b -> p (a b)"), in0=selm[:].rearrange("p a b -> p (a b)"), in1=own[:], op=ALU.max), reads=[selm, own], writes=[selm])
                c.op('dve', lambda e: e.tensor_scalar(out=selm[:], in0=selm[:], scalar1=-1.0, scalar2=-NEGB, op0=ALU.add, op1=ALU.mult), reads=[selm], writes=[selm])
                for q4 in range(8):
                    pT = self.ps()
                    for j in range(4):
                        qt = q4 * 4 + j
                        c.op('pe', lambda e: e.transpose(pT[0:16, j * 128:(j + 1) * 128], selm[:, qt, :], self.idf[:]), reads=[selm, self.idf], writes=[pT], inc=(j == 3))
                    c.op('act', lambda e: e.copy(out=selbT[:, q4 * 512:(q4 + 1) * 512], in_=pT[0:16, :]), reads=[pT], writes=[selbT.sub(q4)])

                def bias_fn(i, j):
                    bl = [(E16[:, i * 128:(i + 1) * 128], selbT[:, j * 512:(j + 1) * 512], [E16, selbT.sub(j)])]
                    if i >= 4 * j:
                        bl.append((self.idb[:], caus[:, i - 4 * j, :], [self.idb, caus]))
                    return bl

                def out_fn(j, po, rd, h=h):
                    o = ost[kk[0] % 2]; kk[0] += 1
                    c.op('dve', lambda e: e.tensor_tensor(out=o[:], in0=po[:], in1=rd[:], op=ALU.mult), reads=[po, rd], writes=[o])
                    c.dma('sp', oT[h * 128:(h + 1) * 128, j * 512:(j + 1) * 512], o[:], reads=[o])
                self.attention(es, qT, lambda i: (kT[:, i * 128:(i + 1) * 128], [kT], 128), lambda i: (vt[:, i, :], [vt]),
                               lambda j: list(range(4 * j + 4)), SC, bias_fn, out_fn)
        c.barrier()
        for g in range(2):
            with ExitStack() as es:
                kcc = self.sb(es, [128, 256], BF16); vcc = self.sb(es, [128, 2, 128], BF16)
                with ExitStack() as es2:
                    src = self.sb(es2, [128, S], BF16); kA = self.sb(es2, [128, S], BF16); kB = self.sb(es2, [128, S], BF16)
                    pe = self.sb(es2, [128, 32]); w1 = self.sb(es2, [128, 32, 128], BF16); w2 = self.sb(es2, [128, 128], BF16)
                    hc = self.sb(es2, [128, 256], BF16)
                    for which in range(2):
                        srcD = (nkcT, nvcT)[which]; peD = (I["cmp_pe_k"], I["cmp_pe_v"])[which][e_]
                        w1D = (I["cmp_w1_k"], I["cmp_w1_v"])[which][e_]; w2D = (I["cmp_w2_k"], I["cmp_w2_v"])[which][e_]
                        c.dma('sp', src[:], srcD[g * 128:(g + 1) * 128, :], writes=[src])
                        c.dma('sp', pe[:], peD.rearrange("t d -> d t"), writes=[pe])
                        c.dma('pool', w1[:], w1D.rearrange("(t d) j -> d t j", d=128), writes=[w1])
                        c.dma('pool', w2[:], w2D, writes=[w2])
                        sv = src[:].rearrange("p (m r) -> p m r", r=16)
                        c.op('dve', lambda e: e.tensor_tensor(out=kA[:].rearrange("p (m r) -> p m r", r=16), in0=sv, in1=pe[:, 0:16].unsqueeze(1).to_broadcast([128, 256, 16]), op=ALU.add),
                             reads=[src, pe], writes=[kA])
                        c.op('dve', lambda e: e.tensor_tensor(out=kB[:].rearrange("p (m r) -> p m r", r=16), in0=sv, in1=pe[:, 16:32].unsqueeze(1).to_broadcast([128, 256, 16]), op=ALU.add),
                             reads=[src, pe], writes=[kB])
                        ph = self.ps()
                        kAv = kA[:].rearrange("p (m r) -> p m r", r=16); kBv = kB[:].rearrange("p (m r) -> p m r", r=16)
                        for t in range(32):
                            rhs = kAv[:, 0:255, t] if t < 16 else kBv[:, 1:256, t - 16]
                            c.op('pe', lambda e: e.matmul(ph[:, 0:255], w1[:, t, :], rhs, start=(t == 0), stop=(t == 31)), reads=[w1, kA, kB], writes=[ph], inc=(t == 31))
                        c.op('dve', lambda e: e.memset(hc[:], 0.0), writes=[hc])
                        c.op('act', lambda e: e.activation(out=hc[:, 0:255], in_=ph[:, 0:255], func=AF.Silu), reads=[ph], writes=[hc])
                        if which == 0:
                            pk = self.ps()
                            c.op('pe', lambda e: e.matmul(pk[:, 0:256], w2[:], hc[:], start=True, stop=True), reads=[w2, hc], writes=[pk])
                            c.op('act', lambda e: e.copy(out=kcc[:], in_=pk[:, 0:256]), reads=[pk], writes=[kcc])
                        else:
                            pv = self.ps()
                            for nt in range(2):
                                c.op('pe', lambda e: e.matmul(pv[:, nt * 128:(nt + 1) * 128], hc[:, nt * 128:(nt + 1) * 128], w2[:], start=True, stop=True), reads=[w2, hc], writes=[pv], inc=(nt == 1))
                            c.op('act', lambda e: e.copy(out=vcc[:].rearrange("p a b -> p (a b)"), in_=pv[:, 0:256]), reads=[pv], writes=[vcc])
                c.barrier()
                qT = self.sb(es, [128, S], BF16); kT = self.sb(es, [128, S], BF16); vt = self.sb(es, [128, 32, 128], BF16); vf = self.sb(es, [128, S])
                vt2 = self.sb(es, [128, 32, 128], BF16); kT2 = self.sb(es, [128, S], BF16)
                cmpb = self.sb(es, [128, 2, S], BF16); cover = self.sb(es, [128, 2, 64], BF16); winb = self.sb(es, [128, 8, 512], BF16)
                caus = self.sb(es, [128, 4, 512], BF16); E64 = self.sb(es, [64, S], BF16); selbT = self.sb(es, [64, S], BF16)
                impacc = self.sb(es, [64, S]); ngs = self.sb(es, [24, S]); gsel = self.sb(es, [24, 3072])
                keep = self.sb(es, [128, 32, 64]); addm = self.sb(es, [128, 32, 64])
                hacc = self.sb(es, [128, S])
                sc_ = [self.sb(es, [128, 512]) for _ in range(2)]; tmp = [self.sb(es, [128, 512]) for _ in range(2)]
                ocb = [self.sb(es, [128, 512]) for _ in range(2)]; ost = [self.sb(es, [128, 512], BF16) for _ in range(2)]
                iv = self.sb(es, [128, 8, 64]); iv2 = self.sb(es, [128, 64]); t8a = self.sb(es, [128, 8]); t8b = self.sb(es, [128, 8]); th = self.sb(es, [128, 1])
                sb_ = self.sb(es, [128, 8, 64])
                c.dma('pool', cmpb[:], I["c_cmpbias"].rearrange("i n q -> n i q"), writes=[cmpb])
                c.dma('pool', cover[:], I["c_cover"].rearrange("i n j -> n i j"), writes=[cover])
                c.dma('pool', winb[:], I["c_winbias"].rearrange("m k q -> k m q"), writes=[winb])
                c.dma('pool', caus[:], I["c_causal"].rearrange("m k q -> k m q"), writes=[caus])
                c.dma('pool', E64[:], I["c_E64"], writes=[E64])
                c.dma('sp', ngs[:], ngT, writes=[ngs]); c.dma('sp', gsel[:], I["c_gsel"], writes=[gsel])
                c.dma('sp', keep[:], I["c_selkeep"], writes=[keep]); c.dma('sp', addm[:], I["c_seladd"], writes=[addm])
                kk = [0]

                def gate_scale(hh, br, j, rd):
                    col = hh * 3 + br
                    pgt = self.ps()
                    c.op('pe', lambda e: e.matmul(pgt[:], gsel[:, col * 128:(col + 1) * 128], ngs[:, j * 512:(j + 1) * 512], start=True, stop=True), reads=[gsel, ngs], writes=[pgt])
                    s_ = sc_[kk[0] % 2]
                    c.op('dve', lambda e: e.tensor_tensor(out=s_[:], in0=pgt[:], in1=rd[:], op=ALU.mult), reads=[pgt, rd], writes=[s_])
                    return s_

                for r in range(4):
                    hh = g * 4 + r
                    c.dma('sp', qT[:], nqT[hh * 128:(hh + 1) * 128, :], writes=[qT])
                    st = {}

                    def extra_fn(i, j, ui, nu, pt, nk):
                        if ui == 0:
                            st['pimp'] = self.ps()
                        c.op('pe', lambda e: e.matmul(st['pimp'][0:64, :], cover[:, i, :], pt[:], start=(ui == 0), stop=(ui == nu - 1)), reads=[cover, pt], writes=[st['pimp']], inc=(ui == nu - 1))

                    def out_fn(j, po, rd, r=r, hh=hh):
                        pimp = st['pimp']
                        jb = impacc.sub(j)
                        if r == 0:
                            c.op('dve', lambda e: e.tensor_tensor(out=impacc[:, j * 512:(j + 1) * 512], in0=pimp[0:64, :], in1=rd[0:64, :], op=ALU.mult), reads=[pimp, rd], writes=[jb])
                        else:
                            t_ = tmp[kk[0] % 2]
                            c.op('dve', lambda e: e.tensor_tensor(out=t_[0:64, :], in0=pimp[0:64, :], in1=rd[0:64, :], op=ALU.mult), reads=[pimp, rd], writes=[t_])
                            c.op('pool', lambda e: e.tensor_tensor(out=impacc[:, j * 512:(j + 1) * 512], in0=impacc[:, j * 512:(j + 1) * 512], in1=t_[0:64, :], op=ALU.add), reads=[t_, jb], writes=[jb])
                        s_ = gate_scale(hh, 0, j, rd)
                        o = ocb[kk[0] % 2]; kk[0] += 1
                        c.op('dve', lambda e: e.tensor_tensor(out=o[:], in0=po[:], in1=s_[:], op=ALU.mult), reads=[po, s_], writes=[o])
                        c.dma('sp', ocT[hh * 128:(hh + 1) * 128, j * 512:(j + 1) * 512], o[:], reads=[o])
                    self.attention(es, qT, lambda i: (kcc[:, i * 128:(i + 1) * 128], [kcc], 128), lambda i: (vcc[:, i, :], [vcc]),
                                   lambda j: [0] if j < 4 else [0, 1], SC, lambda i, j: [(self.idb[:], cmpb[:, i, j * 512:(j + 1) * 512], [self.idb, cmpb])],
                                   out_fn, extra_fn=extra_fn, clamp=True)
                for q8 in range(4):
                    pI = self.ps()
                    for j in range(8):
                        qt = q8 * 8 + j
                        c.op('pe', lambda e: e.transpose(pI[:, j * 64:(j + 1) * 64], impacc[:, qt * 128:(qt + 1) * 128], self.idf[0:64, 0:64]), reads=[impacc, self.idf], writes=[pI], inc=(j == 7))
                    c.op('dve', lambda e: e.tensor_tensor(out=iv[:], in0=pI[:].rearrange("p (a b) -> p a b", b=64), in1=keep[:, q8 * 8:(q8 + 1) * 8, :], op=ALU.mult), reads=[pI, keep], writes=[iv])
                    c.op('dve', lambda e: e.tensor_tensor(out=iv[:], in0=iv[:], in1=addm[:, q8 * 8:(q8 + 1) * 8, :], op=ALU.add), reads=[iv, addm], writes=[iv])
                    for j in range(8):
                        c.op('dve', lambda e: e.max(out=t8a[:], in_=iv[:, j, :]), reads=[iv], writes=[t8a])
                        c.op('dve', lambda e: e.match_replace(out=iv2[:], in_to_replace=t8a[:], in_values=iv[:, j, :], imm_value=-3e38), reads=[iv, t8a], writes=[iv2])
                        c.op('dve', lambda e: e.max(out=t8b[:], in_=iv2[:]), reads=[iv2], writes=[t8b])
                        c.op('dve', lambda e: e.tensor_scalar_max(out=th[:], in0=t8b[:, 7:8], scalar1=-1e29), reads=[t8b], writes=[th])
                        c.op('dve', lambda e: e.tensor_scalar(out=sb_[:, j, :], in0=iv[:, j, :], scalar1=th[:, 0:1], scalar2=None, op0=ALU.is_ge), reads=[iv, th], writes=[sb_])
                    c.op('dve', lambda e: e.tensor_scalar(out=sb_[:], in0=sb_[:], scalar1=-1.0, scalar2=-NEGB, op0=ALU.add, op1=ALU.mult), reads=[sb_], writes=[sb_])
                    for j2 in range(2):
                        pT = self.ps()
                        for j in range(4):
                            c.op('pe', lambda e: e.transpose(pT[0:64, j * 128:(j + 1) * 128], sb_[:, j2 * 4 + j, :], self.idf[:]), reads=[sb_, self.idf], writes=[pT], inc=(j == 3))
                        q4 = q8 * 2 + j2
                        c.op('act', lambda e: e.copy(out=selbT[:, q4 * 512:(q4 + 1) * 512], in_=pT[0:64, :]), reads=[pT], writes=[selbT.sub(q4)])
                c.dma('sp', kT[:], nksT[g * 128:(g + 1) * 128, :], writes=[kT])
                self.load_tokmajor(nvsT[g * 128:(g + 1) * 128, :], S, vt, vf)
                c.dma('sp', kT2[:], nkwT[g * 128:(g + 1) * 128, :], writes=[kT2])
                self.load_tokmajor(nvwT[g * 128:(g + 1) * 128, :], S, vt2, vf)
                for r in range(4):
                    hh = g * 4 + r
                    c.dma('sp', qT[:], nqrT[hh * 128:(hh + 1) * 128, :], writes=[qT])

                    def bias_sel(i, j):
                        bl = [(E64[:, i * 128:(i + 1) * 128], selbT[:, j * 512:(j + 1) * 512], [E64, selbT.sub(j)])]
                        if i >= 4 * j:
                            bl.append((self.idb[:], caus[:, i - 4 * j, :], [self.idb, caus]))
                        return bl

                    def out_sel(j, po, rd, hh=hh):
                        s_ = gate_scale(hh, 1, j, rd); kk[0] += 1
                        c.op('dve', lambda e: e.tensor_tensor(out=hacc[:, j * 512:(j + 1) * 512], in0=po[:], in1=s_[:], op=ALU.mult), reads=[po, s_], writes=[hacc.sub(j)])
                    self.attention(es, qT, lambda i: (kT[:, i * 128:(i + 1) * 128], [kT], 128), lambda i: (vt[:, i, :], [vt]),
                                   lambda j: list(range(4 * j + 4)), SC, bias_sel, out_sel)

                    def out_win(j, po, rd, hh=hh):
                        s_ = gate_scale(hh, 2, j, rd)
                        t_ = tmp[kk[0] % 2]; o = ost[kk[0] % 2]; ob_ = ocb[kk[0] % 2]; kk[0] += 1
                        c.dma('sp', ob_[:], ocT[hh * 128:(hh + 1) * 128, j * 512:(j + 1) * 512], writes=[ob_])
                        c.op('dve', lambda e: e.tensor_tensor(out=t_[:], in0=po[:], in1=s_[:], op=ALU.mult), reads=[po, s_], writes=[t_])
                        c.op('pool', lambda e: e.tensor_tensor(out=t_[:], in0=t_[:], in1=hacc[:, j * 512:(j + 1) * 512], op=ALU.add), reads=[t_, hacc.sub(j)], writes=[t_])
                        c.op('pool', lambda e: e.tensor_tensor(out=o[:], in0=t_[:], in1=ob_[:], op=ALU.add), reads=[t_, ob_], writes=[o])
                        c.dma('sp', oT[1024 + hh * 128:1024 + (hh + 1) * 128, j * 512:(j + 1) * 512], o[:], reads=[o])
                    self.attention(es, qT, lambda i: (kT2[:, i * 128:(i + 1) * 128], [kT2], 128), lambda i: (vt2[:, i, :], [vt2]),
                                   lambda j: list(range(max(0, 4 * j - 4), 4 * j + 4)), SC,
                                   lambda i, j: [(self.idb[:], winb[:, i - 4 * j + 4, :], [self.idb, winb])], out_win)
            c.barrier()
        with ExitStack() as es:
            self.linear(I["w_out_ab"][e_], 16, S, 2048, [[4 * i + j for j in range(4)] for i in range(4)], self.epi_residual(es, hT), src=oT)

    def cross_attn(self, l):
        c = self.c; I = self.inp
        hT = self.scr["hT"]
        xqT = self.dscr("xqT", [512, S], BF16); kxT = self.dscr("kxT", [512, NMEM], BF16)
        vxT = self.dscr("vxT", [512, NMEM], F32); oxT = self.dscr("oxT", [512, S], BF16)
        with ExitStack() as es:
            self.linear(I["w_q_x"][l], 16, S, 2048, [[0, 1, 2, 3]], self.epi_store(es, xqT, BF16), norm=(hT, I["norm_cross"][l]))
        with ExitStack() as es:
            ek = self.epi_store(es, kxT, BF16); ev = self.epi_store(es, vxT, F32)

            def epi(cc, t0, TB, p):
                if cc < 4:
                    ek(cc, t0, TB, p)
                else:
                    ev(cc, t0, TB, p, row0=(cc - 4) * 128)
            self.linear(I["w_kv_x"][l], 16, NMEM, NMEM, [[0, 1, 2, 3], [4, 5, 6, 7]], epi, norm=(self.scr["memT"], I["norm_mem"]))
        with ExitStack() as es:
            qT = self.sb(es, [128, S], BF16); kT = self.sb(es, [128, NMEM], BF16); vt = self.sb(es, [128, 2, 128], BF16); vfx = self.sb(es, [128, NMEM])
            ost = [self.sb(es, [128, 512], BF16) for _ in range(2)]
            k = [0]
            for h in range(4):
                c.dma('sp', qT[:], xqT[h * 128:(h + 1) * 128, :], writes=[qT])
                c.dma('sp', kT[:], kxT[h * 128:(h + 1) * 128, :], writes=[kT])
                self.load_tokmajor(vxT[h * 128:(h + 1) * 128, :], NMEM, vt, vfx)

                def out_fn(j, po, rd, h=h):
                    o = ost[k[0] % 2]; k[0] += 1
                    c.op('dve', lambda e: e.tensor_tensor(out=o[:], in0=po[:], in1=rd[:], op=ALU.mult), reads=[po, rd], writes=[o])
                    c.dma('sp', oxT[h * 128:(h + 1) * 128, j * 512:(j + 1) * 512], o[:], reads=[o])
                self.attention(es, qT, lambda i: (kT[:, i * 128:(i + 1) * 128], [kT], 128), lambda i: (vt[:, i, :], [vt]),
                               lambda j: [0, 1], 128 ** -0.5, lambda i, j: [], out_fn)
        c.barrier()
        with ExitStack() as es:
            self.linear(I["w_o_x"][l], 4, S, 2048, [[0, 1, 2, 3], [4, 5, 6, 7], [8, 9, 10, 11], [12, 13, 14, 15]], self.epi_residual(es, hT), src=oxT)

    def ffn(self, l):
        c = self.c; I = self.inp
        hT = self.scr["hT"]
        aT = self.dscr("aT", [DFF, S], BF16)
        NT = 2048
        with ExitStack() as es:
            cw = self.sb(es, [128, 3, 88]); cb = self.sb(es, [128, 88])
            for i3 in range(3):
                c.dma('sp', cw[:, i3, :], I["conv_w"][l][i3].rearrange("(k p) -> p k", p=128), writes=[cw])
            c.dma('sp', cb[:], I["conv_b"][l].rearrange("(k p) -> p k", p=128), writes=[cb])
            halo = self.sb(es, [128, 88, 2])
            c.op('dve', lambda e: e.memset(halo[:], 0.0), writes=[halo])
            U = [self.sb(es, [128, 514]) for _ in range(2)]
            cv = [self.sb(es, [128, 512]) for _ in range(3)]
            gst = self.sb(es, [128, 2, NT])
            ast = [self.sb(es, [128, 512], BF16) for _ in range(2)]
            k = [0]

            pend = [None]; ka = [0]

            def flush():
                if pend[0] is not None:
                    f = pend[0]; pend[0] = None
                    f()

            def epi(cc, t0, TB, p):
                u = U[k[0] % 2]; v = cv[k[0] % 3]; k[0] += 1
                tl = t0 % NT
                c.op('act', lambda e: e.copy(out=u[:, 2:514], in_=p[:, :]), reads=[p], writes=[u.sub('m')])
                flush()
                c.op('pool', lambda e: e.tensor_copy(out=u[:, 0:2], in_=halo[:, cc, :]), reads=[halo.sub(cc)], writes=[u.sub('h')])
                c.op('pool', lambda e: e.tensor_copy(out=halo[:, cc, :], in_=u[:, 512:514]), reads=[u.sub('m')], writes=[halo.sub(cc)])
                c.op('dve', lambda e: e.tensor_scalar(out=v[:], in0=u[:, 2:514], scalar1=cw[:, 2, cc:cc + 1], scalar2=cb[:, cc:cc + 1], op0=ALU.mult, op1=ALU.add),
                     reads=[u.sub('m'), cw, cb], writes=[v])
                c.op('dve', lambda e: e.scalar_tensor_tensor(out=v[:], in0=u[:, 1:513], scalar=cw[:, 1, cc:cc + 1], in1=v[:], op0=ALU.mult, op1=ALU.add),
                     reads=[u, cw, v], writes=[v])
                c.op('dve', lambda e: e.scalar_tensor_tensor(out=v[:], in0=u[:, 0:512], scalar=cw[:, 0, cc:cc + 1], in1=v[:], op0=ALU.mult, op1=ALU.add),
                     reads=[u, cw, v], writes=[v])

                def tail():
                    if cc < 44:
                        c.op('act', lambda e: e.activation(out=gst[:, cc % 2, tl:tl + 512], in_=v[:], func=AF.Silu), reads=[v], writes=[gst.sub((cc % 2, tl))])
                    else:
                        a = ast[ka[0] % 2]; ka[0] += 1
                        c.op('pool', lambda e: e.tensor_tensor(out=a[:], in0=gst[:, cc % 2, tl:tl + 512], in1=v[:], op=ALU.mult), reads=[gst.sub((cc % 2, tl)), v], writes=[a])
                        c.dma('sp', aT[(cc - 44) * 128:(cc - 43) * 128, t0:t0 + 512], a[:], reads=[a])
                pend[0] = tail
            epi.flush = flush
            groups = [[2 * pp, 2 * pp + 1, 44 + 2 * pp, 45 + 2 * pp] for pp in range(22)]
            self.linear(I["w_up"][l], 16, S, NT, groups, epi, norm=(hT, I["norm_ffn"][l]), CGMAX=512)
        with ExitStack() as es:
            self.linear(I["w_down"][l], 44, S, 1024, [[2 * i, 2 * i + 1] for i in range(8)], self.epi_residual(es, hT), src=aT, CGMAX=256)

    def final(self):
        c = self.c; I = self.inp
        hT = self.scr["hT"]; out = self.out
        with ExitStack() as es:
            XT = self.sb(es, [128, 16, 256], F32)
            st = {}
            ot = [self.sb(es, [128, D]) for _ in range(2)]
            for blk in range(S // 256):
                self.norm_input(es, hT, I["norm_final"], blk * 256, 256, XT, st)
                for t2 in range(2):
                    o = ot[(blk * 2 + t2) % 2]
                    for q4 in range(4):
                        p = self.ps()
                        for j in range(4):
                            kc = q4 * 4 + j
                            c.op('pe', lambda e, p=p, j=j, kc=kc: e.transpose(p[:, j * 128:(j + 1) * 128], XT[:, kc, t2 * 128:(t2 + 1) * 128], self.idf[:]),
                                 reads=[XT, self.idf], writes=[p], inc=(j == 3))
                        c.op('act', lambda e, p=p, q4=q4: e.copy(out=o[:, q4 * 512:(q4 + 1) * 512], in_=p[:]), reads=[p], writes=[o])
                    tok = blk * 256 + t2 * 128
                    c.dma('sp', out[tok:tok + 128, :], o[:], reads=[o])
        c.barrier()

    def build(self):
        nc = self.nc
        I = self.inp
        self.din("x", [S, D]); self.din("mem", [NMEM, D]); self.din("positions", [S], I32)
        for n in ("norm_mix", "norm_cross", "norm_ffn"):
            self.din(n, [DEPTH, D])
        self.din("norm_mem", [D]); self.din("norm_final", [D])
        self.din("w_q_x", [DEPTH, D, 512]); self.din("w_kv_x", [DEPTH, D, 1024]); self.din("w_o_x", [DEPTH, 512, D])
        self.din("w_up", [DEPTH, D, 2 * DFF]); self.din("conv_w", [DEPTH, 3, 2 * DFF]); self.din("conv_b", [DEPTH, 2 * DFF])
        self.din("w_down", [DEPTH, DFF, D])
        self.din("c_ident", [128, 128])
        self.din("w_in_c", [2, D, 12288]); self.din("ret_gn", [2, 8, 512]); self.din("w_out_c", [2, 4096, D])
        self.din("w_in_ab", [2, D, 5656]); self.din("w_out_ab", [2, D, D])
        for nm in ("cmp_pe_k", "cmp_pe_v"):
            self.din(nm, [2, 32, 128])
        for nm in ("cmp_w1_k", "cmp_w1_v"):
            self.din(nm, [2, 4096, 128])
        for nm in ("cmp_w2_k", "cmp_w2_v"):
            self.din(nm, [2, 128, 128])
        self.din("c_perm", [32, 32]); self.din("c_inv32", [32, 1]); self.din("c_mobavalid", [128, 512]); self.din("c_mobaown", [128, 512])
        self.din("c_E16", [16, S]); self.din("c_causal", [4, 128, 512]); self.din("c_cmpbias", [2, 128, S]); self.din("c_cover", [2, 128, 64])
        self.din("c_gsel", [24, 3072]); self.din("c_selkeep", [128, 32, 64]); self.din("c_seladd", [128, 32, 64]); self.din("c_E64", [64, S])
        self.din("c_winbias", [8, 128, 512])
        self.din("c_decT", [8, 128, 128]); self.din("c_qdec", [8, 128, 128]); self.din("c_kdec", [128, 8]); self.din("c_invR", [128, 1])
        self.out = nc.dram_tensor("out", [S, D], F32, kind="ExternalOutput").ap()
        self.dscr("hT", [D, S], F32); self.dscr("memT", [D, NMEM], F32)
        self.setup()
        self.to_feature_major(I["x"], self.scr["hT"], S)
        self.to_feature_major(I["mem"], self.scr["memT"], NMEM)
        self.dscr("cosR", [128, S], F32); self.dscr("sinR", [128, S], F32)
        self.rope_tables(I["c_invR"], 128, self.scr["cosR"], self.scr["sinR"])
        self.dscr("cos32", [32, S], F32); self.dscr("sin32", [32, S], F32)
        self.rope_tables(I["c_inv32"], 32, self.scr["cos32"], self.scr["sin32"])
        for l in range(self.nlayers):
            if "mix" in self.parts:
                if l % 2 == 1:
                    self.mixer_c(l)
                else:
                    self.mixer_ab(l)
            if "x" in self.parts:
                self.cross_attn(l)
            if "ffn" in self.parts:
                self.ffn(l)
        self.final()
        self.c.barrier()
        self.es.close()
        return nc


def host_consts():
    cs = {"c_ident": np.eye(128, dtype=np.float32)}
    n = np.arange(128, dtype=np.float64)
    decT = np.zeros((8, 128, 128)); qdec = np.zeros((8, 128, 128)); kdec = np.zeros((128, 8))
    for h in range(8):
        lg = np.log(1.0 - 2.0 ** (-5.0 - h))
        diff = n[None, :] - n[:, None]
        decT[h] = np.where(diff >= 0, np.exp(np.maximum(diff, 0) * lg), 0.0) / 16.0
        qdec[h] = np.exp((n + 1.0) * lg)[None, :].repeat(128, 0)
        kdec[:, h] = np.exp((127.0 - n) * lg) / 16.0
    cs["c_decT"] = decT.astype(np.float32); cs["c_qdec"] = qdec.astype(np.float32); cs["c_kdec"] = kdec.astype(np.float32)
    pm = np.zeros((32, 32), np.float32)
    for d in range(32):
        pm[(d + 16) % 32, d] = -1.0 if d < 16 else 1.0
    cs["c_perm"] = pm
    cs["c_inv32"] = (500000.0 ** (-(np.arange(32) % 16) / 16.0)).astype(np.float32).reshape(32, 1)
    ql = np.arange(128)[:, None, None]; qt = np.arange(32)[None, :, None]
    n16 = np.arange(16)[None, None, :]
    cs["c_mobavalid"] = np.where(n16 < (qt // 2) + 0 * ql, 0.0, -1e30).astype(np.float32).reshape(128, 512)
    cs["c_mobaown"] = np.where(n16 == (qt // 2) + 0 * ql, 1.0, 0.0).astype(np.float32).reshape(128, 512)
    sidx = np.arange(S)
    cs["c_E16"] = (sidx[None, :] // 256 == np.arange(16)[:, None]).astype(np.float32)
    cs["c_E64"] = (sidx[None, :] // 64 == np.arange(64)[:, None]).astype(np.float32)
    kk_ = np.arange(128)[None, :, None]; qq = np.arange(512)[None, None, :]
    m4 = np.arange(4)[:, None, None]
    cs["c_causal"] = np.where(128 * m4 + kk_ <= qq, 0.0, NEGB).astype(np.float32)
    m8 = (np.arange(8) - 4)[:, None, None]
    dist = qq - (128 * m8 + kk_)
    cs["c_winbias"] = np.where((dist >= 0) & (dist < 512), 0.0, NEGB).astype(np.float32)
    nn = (np.arange(2)[:, None, None] * 128 + np.arange(128)[None, :, None])
    cs["c_cmpbias"] = np.where((16 * nn + 31 <= sidx[None, None, :]) & (nn < 255), 0.0, NEGB).astype(np.float32)
    jj = np.arange(64)[None, None, :]
    cs["c_cover"] = ((nn >= 4 * jj - 1) & (nn <= 4 * jj + 3) & (nn < 255)).astype(np.float32)
    gs = np.zeros((24, 24, 128), np.float32)
    for cidx in range(24):
        gs[cidx, cidx, :] = 1.0
    cs["c_gsel"] = gs.reshape(24, 3072)
    q_ = qt * 128 + ql
    blk = q_ // 64
    j64 = np.arange(64)[None, None, :]
    is0 = (j64 == 0); isb = (j64 == blk); isb1 = (j64 == blk - 1)
    forced = is0 | isb | isb1
    cs["c_selkeep"] = ((~forced) & (j64 <= blk)).astype(np.float32)
    add = np.where(j64 > blk, -1e30, 0.0)
    add = np.where(is0, 1e9, add); add = np.where(isb1, 2e9, add); add = np.where(isb, 3e9, add)
    cs["c_seladd"] = add.astype(np.float32)
    cs["c_invR"] = (10000.0 ** (-np.arange(128) / 128.0)).astype(np.float32).reshape(128, 1)
    return cs


def kernel(**inputs):
    prog = Prog()
    nc = prog.build()
    consts = host_consts()
    in_maps = []
    for core in range(8):
        b = core // 2
        m = {}
        for name in prog.inp:
            if name in consts:
                m[name] = consts[name]
            elif name in ("x", "mem", "positions"):
                m[name] = np.ascontiguousarray(inputs[name][b])
            else:
                m[name] = np.ascontiguousarray(inputs[name])
        in_maps.append(m)
    res = run_bass_kernel_spmd(nc, in_maps, core_ids=list(range(8)))
    out = np.empty((4, S, D), np.float32)
    for b in range(4):
        out[b, :S // 2] = res.results[2 * b]["out"][:S // 2]
        out[b, S // 2:] = res.results[2 * b + 1]["out"][S // 2:]
    return out
```
